# Optimizing a Trainium2 kernel written in Bass

```python
import jax, jax.numpy as jnp
from jax import lax
import numpy as np

D_MODEL = 2048
BATCH = 4
SEQ = 2048
DEPTH = 1

MIX_WIDTH = D_MODEL
POOL_WIDTH = D_MODEL // 2
POOL_WINDOWS = (2, 4, 8, 16)
N_POOL_GROUPS = len(POOL_WINDOWS)
POOL_GROUP_DIM = POOL_WIDTH // N_POOL_GROUPS
HGRN_WIDTH = MIX_WIDTH - POOL_WIDTH
HGRN_EXPAND = 128
HGRN_HEADS = HGRN_WIDTH // HGRN_EXPAND
HGRN_HEAD_I = HGRN_WIDTH // HGRN_HEADS
HGRN_QK = HGRN_HEADS * HGRN_EXPAND
IN_WIDTH = POOL_WIDTH + 2 * HGRN_QK + 2 * HGRN_WIDTH
CHUNK = 64
D_FF = 5632
N_MOD = 9
EPS = 1e-6

kernel_name = "hybrid_pool_hgrn2_macaron_adaln"


def rmsnorm(x, w):
    xf = x.astype(jnp.float32)
    y = xf * lax.rsqrt(jnp.mean(xf * xf, axis=-1, keepdims=True) + EPS)
    return (y * w.astype(jnp.float32)).astype(x.dtype)


def modulate(h, shift, scale):
    return h * (1.0 + scale[:, None, :]) + shift[:, None, :]


def swiglu(h, w_gate, w_up, w_down):
    return (jax.nn.silu(h @ w_gate) * (h @ w_up)) @ w_down


def causal_multiscale_pool(u):
    B, S, G, Cg = u.shape
    uf = u.astype(jnp.float32)
    wmax = max(POOL_WINDOWS)
    cs = jnp.cumsum(uf, axis=1)
    csp = jnp.pad(cs, ((0, 0), (wmax, 0), (0, 0), (0, 0)))
    t = jnp.arange(S)
    means = []
    for g, w in enumerate(POOL_WINDOWS):
        win = csp[:, wmax:, g] - csp[:, wmax - w:wmax - w + S, g]
        cnt = jnp.minimum(t + 1, w).astype(jnp.float32)
        means.append(win / cnt[None, :, None])
    mean = jnp.stack(means, axis=2)
    return (mean - uf).astype(u.dtype)


def hgrn2_chunk_scan(q, k, v, logf):
    B, S, H, Dk = q.shape
    Dv = v.shape[-1]
    N = S // CHUNK

    def to_chunks(a):
        return a.reshape(B, N, CHUNK, H, a.shape[-1]).transpose(1, 0, 3, 2, 4)

    qc, kc, vc, gc = to_chunks(q), to_chunks(k), to_chunks(v), to_chunks(logf)
    causal = jnp.tril(jnp.ones((CHUNK, CHUNK), dtype=bool))

    def step(s_prev, inp):
        qb, kb, vb, gb = inp
        b = jnp.cumsum(gb, axis=2)
        diff = b[:, :, :, None, :] - b[:, :, None, :, :]
        decay = jnp.exp(jnp.where(causal[:, :, None], diff, -jnp.inf))
        attn = jnp.einsum('bhtd,bhsd,bhtsd->bhts', qb, kb, decay)
        o = (jnp.einsum('bhts,bhsv->bhtv', attn, vb)
             + jnp.einsum('bhtd,bhdv->bhtv', qb * jnp.exp(b), s_prev))
        b_last = b[:, :, -1:, :]
        s_new = (jnp.exp(b_last[:, :, 0, :])[..., None] * s_prev
                 + jnp.einsum('bhsd,bhsv->bhdv', kb * jnp.exp(b_last - b), vb))
        return s_new, o

    s0 = jnp.zeros((B, H, Dk, Dv), jnp.float32)
    _, o = lax.scan(step, s0, (qc, kc, vc, gc))
    return o.transpose(1, 0, 3, 2, 4).reshape(B, S, H, Dv)


def setup_inputs(seed: int = 0) -> dict:
    key = jax.random.key(seed)
    ks = jax.random.split(key, 20)
    L, D = DEPTH, D_MODEL
    nrm = jax.random.normal
    return {
        "x": nrm(ks[0], (BATCH, SEQ, D), jnp.float32),
        "c": nrm(ks[1], (BATCH, D), jnp.float32),
        "w_ada": nrm(ks[2], (L, D, N_MOD * D), jnp.float32) * (0.5 * D ** -0.5),
        "b_ada": nrm(ks[3], (L, N_MOD * D), jnp.float32) * 0.01,
        "norm1_w": 1.0 + 0.02 * nrm(ks[4], (L, D), jnp.float32),
        "ffn1_gate": nrm(ks[5], (L, D, D_FF), jnp.float32) * D ** -0.5,
        "ffn1_up": nrm(ks[6], (L, D, D_FF), jnp.float32) * D ** -0.5,
        "ffn1_down": nrm(ks[7], (L, D_FF, D), jnp.float32) * D_FF ** -0.5,
        "norm2_w": 1.0 + 0.02 * nrm(ks[8], (L, D), jnp.float32),
        "w_in": nrm(ks[9], (L, D, IN_WIDTH), jnp.float32) * D ** -0.5,
        "pool_w": nrm(ks[10], (L, N_POOL_GROUPS, POOL_GROUP_DIM, POOL_GROUP_DIM), jnp.float32) * POOL_GROUP_DIM ** -0.5,
        "pool_scale": 1.0 + 0.1 * nrm(ks[11], (L, POOL_WIDTH), jnp.float32),
        "lb_logits": 0.1 * nrm(ks[12], (L + 1, HGRN_QK), jnp.float32),
        "gnorm_w": 1.0 + 0.02 * nrm(ks[13], (L, HGRN_HEAD_I), jnp.float32),
        "w_out": nrm(ks[14], (L, MIX_WIDTH, D), jnp.float32) * MIX_WIDTH ** -0.5,
        "norm3_w": 1.0 + 0.02 * nrm(ks[15], (L, D), jnp.float32),
        "ffn2_gate": nrm(ks[16], (L, D, D_FF), jnp.float32) * D ** -0.5,
        "ffn2_up": nrm(ks[17], (L, D, D_FF), jnp.float32) * D ** -0.5,
        "ffn2_down": nrm(ks[18], (L, D_FF, D), jnp.float32) * D_FF ** -0.5,
        "final_norm_w": 1.0 + 0.02 * nrm(ks[19], (D,), jnp.float32),
    }


def reference(x, c, w_ada, b_ada, norm1_w, ffn1_gate, ffn1_up, ffn1_down, norm2_w, w_in,
              pool_w, pool_scale, lb_logits, gnorm_w, w_out, norm3_w, ffn2_gate, ffn2_up,
              ffn2_down, final_norm_w):
    B, S, _ = x.shape
    p = jax.nn.softmax(lb_logits.astype(jnp.float32), axis=0)
    lb_all = jnp.cumsum(p, axis=0) - p[0]
    c_act = jax.nn.silu(c)
    splits = [POOL_WIDTH, POOL_WIDTH + HGRN_QK, POOL_WIDTH + 2 * HGRN_QK,
              POOL_WIDTH + 2 * HGRN_QK + HGRN_WIDTH]

    for l in range(DEPTH):
        mod = c_act @ w_ada[l] + b_ada[l]
        sh1, sc1, g1, sh2, sc2, g2, sh3, sc3, g3 = jnp.split(mod, N_MOD, axis=-1)

        h = modulate(rmsnorm(x, norm1_w[l]), sh1, sc1)
        x = x + 0.5 * g1[:, None, :] * swiglu(h, ffn1_gate[l], ffn1_up[l], ffn1_down[l])

        h = modulate(rmsnorm(x, norm2_w[l]), sh2, sc2)
        z = h @ w_in[l]
        zp, zq, zf, zi, zg = jnp.split(z, splits, axis=-1)

        u = zp.reshape(B, S, N_POOL_GROUPS, POOL_GROUP_DIM)
        pooled = causal_multiscale_pool(u)
        y_pool = (jnp.einsum('bsgc,gcd->bsgd', pooled, pool_w[l]).reshape(B, S, POOL_WIDTH)
                  * pool_scale[l])

        lb = lb_all[l + 1]
        forget = lb + (1.0 - lb) * jax.nn.sigmoid(zf.astype(jnp.float32))
        qh = jax.nn.silu(zq.astype(jnp.float32)).reshape(B, S, HGRN_HEADS, HGRN_EXPAND)
        kh = (1.0 - forget).reshape(B, S, HGRN_HEADS, HGRN_EXPAND)
        gh = jnp.log(forget).reshape(B, S, HGRN_HEADS, HGRN_EXPAND)
        vh = zi.astype(jnp.float32).reshape(B, S, HGRN_HEADS, HGRN_HEAD_I)
        o = hgrn2_chunk_scan(qh, kh, vh, gh)
        o = rmsnorm(o, gnorm_w[l]) * jax.nn.silu(
            zg.astype(jnp.float32).reshape(B, S, HGRN_HEADS, HGRN_HEAD_I))
        y_hgrn = o.reshape(B, S, HGRN_WIDTH).astype(x.dtype)

        mix = jnp.concatenate([y_pool.astype(x.dtype), y_hgrn], axis=-1) @ w_out[l]
        x = x + g2[:, None, :] * mix

        h = modulate(rmsnorm(x, norm3_w[l]), sh3, sc3)
        x = x + 0.5 * g3[:, None, :] * swiglu(h, ffn2_gate[l], ffn2_up[l], ffn2_down[l])

    return rmsnorm(x, final_norm_w)
```

```python
from contextlib import ExitStack
import numpy as np
import concourse.bass as bass
import concourse.mybir as mybir
from concourse.bass_utils import run_bass_kernel_spmd

F32 = mybir.dt.float32
BF16 = mybir.dt.bfloat16
AF = mybir.ActivationFunctionType
ALU = mybir.AluOpType

D = 2048
S = 2048
T = 1024
KC = 16
DFF = 5632
NFF = 44
EPS = 1e-6
WIN = (2, 4, 8, 16)
FF_GROUPS = ((0, 16), (16, 16), (32, 12))


class Op:
    __slots__ = ("eng", "fn", "deps", "dma", "idx", "sem", "val", "signal", "pre")

    def __init__(self, eng, fn, deps, dma, idx):
        self.eng = eng
        self.fn = fn
        self.deps = deps
        self.dma = dma
        self.idx = idx
        self.sem = None
        self.val = 0
        self.signal = False
        self.pre = None


class _Rec:
    def __init__(self):
        self.call = None

    def __getattr__(self, name):
        def f(*a, **k):
            self.call = (name, a, k)
            return self
        return f


class Prog:
    ENGS = ("pe", "act", "dve", "pool", "sp")
    NDMASEM = 24

    def __init__(self):
        self.ops = []
        self.last_w = {}
        self.readers = {}
        self.touched = {}
        self.fence = {}

    def switch(self, slab):
        t = self.touched.get(slab, set())
        last = {}
        f = set()
        for i in t:
            op = self.ops[i]
            if op.dma:
                f.add(i)
            else:
                if op.eng not in last or last[op.eng] < i:
                    last[op.eng] = i
        f.update(last.values())
        f |= self.fence.get(slab, set()) if not t else set()
        self.fence[slab] = f
        self.touched[slab] = set()
        for d in (self.last_w, self.readers):
            for k in [k for k in d if k[0] == slab]:
                del d[k]

    def add(self, eng, fn, r=(), w=(), dma=False):
        idx = len(self.ops)
        deps = set()
        for k in r:
            lw = self.last_w.get(k)
            if lw is not None:
                deps.add(lw)
            elif k[0] in self.fence:
                deps.update(self.fence[k[0]])
        for k in w:
            lw = self.last_w.get(k)
            if lw is not None:
                deps.add(lw)
            elif k[0] in self.fence:
                deps.update(self.fence[k[0]])
            rd = self.readers.get(k)
            if rd:
                lastc = {}
                for i in rd:
                    o_ = self.ops[i]
                    if o_.dma:
                        deps.add(i)
                    elif lastc.get(o_.eng, -1) < i:
                        lastc[o_.eng] = i
                deps.update(lastc.values())
        for k in r:
            self.readers.setdefault(k, []).append(idx)
            self.touched.setdefault(k[0], set()).add(idx)
        for k in w:
            self.last_w[k] = idx
            self.readers[k] = []
            self.touched.setdefault(k[0], set()).add(idx)
        if fn is not None:
            rec = _Rec()
            fn(rec)
            fn = rec.call
        op = Op(eng, fn, deps, dma, idx)
        self.ops.append(op)
        return op

    def emit(self, nc, stack):
        ops = self.ops
        for op in ops:
            for d in op.deps:
                dop = ops[d]
                if dop.dma:
                    dop.signal = True
                elif dop.eng != op.eng or op.eng != "pe" or op.dma:
                    dop.signal = True
        esem = {e: stack.enter_context(nc.semaphore("s_" + e)) for e in ("pe", "act", "dve", "pool")}
        ndma = {e: sum(1 for op in ops if op.dma and op.dma != "cc" and op.eng == e) for e in ("pool", "sp")}
        nds = {e: max(2, -(-ndma[e] // 13)) for e in ndma}
        dsem = {e: [stack.enter_context(nc.semaphore("d_%s%d" % (e, i))) for i in range(nds[e])]
                for e in ("pool", "sp")}
        cnt = {e: 0 for e in esem}
        dcnt = {e: [0] * nds[e] for e in dsem}
        drr = {e: 0 for e in dsem}
        ccsem = None
        ccn = 0
        for op in ops:
            if op.dma == "cc":
                if ccsem is None:
                    ccsem = stack.enter_context(nc.semaphore("s_cc"))
                ccn += 1
                op.sem = ccsem
                op.val = ccn
                op.signal = True
            elif op.dma:
                e = op.eng
                i = drr[e]
                drr[e] = (i + 1) % nds[e]
                if dcnt[e][i] > 0:
                    op.pre = (dsem[e][i], dcnt[e][i])
                dcnt[e][i] += 16
                op.sem = dsem[e][i]
                op.val = dcnt[e][i]
                op.signal = True
            elif op.signal:
                cnt[op.eng] += 1
                op.sem = esem[op.eng]
                op.val = cnt[op.eng]
        by_eng = {e: [op for op in ops if op.eng == e] for e in self.ENGS}
        nwaits = {e: 0 for e in self.ENGS}

        def run(engname, eng):
            seen = {}
            for op in by_eng[engname]:
                waits = []
                if op.pre is not None:
                    waits.append(op.pre)
                for d in op.deps:
                    dop = ops[d]
                    if not dop.signal:
                        continue
                    if (not dop.dma) and dop.eng == engname and engname == "pe" and not op.dma:
                        continue
                    waits.append((dop.sem, dop.val))
                best = {}
                for s, v in waits:
                    k = id(s)
                    if seen.get(k, 0) >= v:
                        continue
                    if k not in best or best[k][1] < v:
                        best[k] = (s, v)
                for k, (s, v) in best.items():
                    eng.wait_ge(s, v)
                    seen[k] = v
                    nwaits[engname] += 1
                if op.fn is not None:
                    name, a_, k_ = op.fn
                    ins = getattr(eng, name)(*a_, **k_)
                    if op.signal:
                        ins.then_inc(op.sem, 16 if (op.dma and op.dma != "cc") else 1)

        with nc.Block() as block:
            @block.tensor
            def _(e):
                run("pe", e)

            @block.scalar
            def _(e):
                run("act", e)

            @block.vector
            def _(e):
                run("dve", e)

            @block.gpsimd
            def _(e):
                run("pool", e)

            @block.sync
            def _(e):
                run("sp", e)
        self.stats = dict(nops={e: len(by_eng[e]) for e in self.ENGS}, nwaits=nwaits, nsig=cnt)


class Arena:
    def __init__(self, handle, nbytes):
        self.h = handle
        self.n = nbytes
        self.off = 0

    def reset(self):
        self.off = 0

    def alloc(self, free_shape, dt):
        esz = 4 if dt == F32 else 2
        n = int(np.prod(free_shape))
        nb = n * esz
        self.off = (self.off + 31) // 32 * 32
        assert self.off + nb <= self.n, ("arena overflow", self.off, nb, self.n)
        a = self.h[:, self.off // 4:(self.off + nb) // 4]
        self.off += nb
        if dt != F32:
            a = a.bitcast(dt)
        if len(free_shape) == 2:
            a = a.rearrange("p (a b) -> p a b", b=free_shape[1])
        elif len(free_shape) == 3:
            a = a.rearrange("p (a b c) -> p a b c", b=free_shape[1], c=free_shape[2])
        return a


def build_nc(nblk=2, stop=99, dbg=False, exch=False):
    nc = bass.Bass("TRN2", target_bir_lowering=False)
    NTOK = nblk * T

    def din(name, shape):
        return nc.dram_tensor(name, list(shape), F32, kind="ExternalInput").ap()

    x_d = din("x", (NTOK, D))
    c_d = din("c", (16, 128))
    w_ada = din("w_ada", (D, 9 * D)) if not exch else None
    b_ada = din("b_ada", (144, 128))
    nrm_d = din("nrm", (4, 16, 128))
    ffn_d = [(din("ffn1_gate", (D, DFF)), din("ffn1_up", (D, DFF)), din("ffn1_down", (DFF, D))),
             (din("ffn2_gate", (D, DFF)), din("ffn2_up", (D, DFF)), din("ffn2_down", (DFF, D)))]
    w_in = din("w_in", (D, 5120))
    pool_w = din("pool_w", (4, 256, 256))
    pscale_d = din("pool_scale", (8, 128))
    lb_d = din("lb_logits", (16, 128))
    gn_d = din("gnorm_w", (1, 128))
    w_out = din("w_out", (D, D))
    ident_d = din("ident", (128, 128))
    cmask_d = din("cmask", (128, 128))
    smask_d = din("smask", (128, 512))
    invcnt_d = din("invcnt", (128, 64))
    sel_d = din("sel", (128, 8))
    if exch:
        call_d = din("c_all", (64, 128))
        wsl_d = din("w_ada_sl", (D, 2304))
        selb_d = din("selb", (128, 4))
        cin2_d = nc.dram_tensor("cin2", [72, 128], F32).ap()
        cout2_d = nc.dram_tensor("cout2", [576, 128], F32).ap()
    cin_d = nc.dram_tensor("cin", [1152, 128], F32).ap()
    cout_d = nc.dram_tensor("cout", [8 * 1152, 128], F32).ap()
    out_d = nc.dram_tensor("out", [NTOK, D], F32, kind="ExternalOutput").ap()
    dbg_d = nc.dram_tensor("dbg", [128, KC, T], F32, kind="ExternalOutput").ap() if dbg else None

    p = Prog()
    st = ExitStack()
    with st:
        def sb(name, shape, dt):
            return st.enter_context(nc.sbuf_tensor("sb_" + name, list(shape), dt))

        xT = sb("xT", [128, KC, T], F32)
        hT = sb("hT", [128, KC, T], BF16)
        slabA_h = sb("slabA", [128, 8192], F32)
        slabA = Arena(slabA_h, 32768)
        NW = 3
        wpool = [sb("wp%d" % i, [128, 4096], BF16) for i in range(NW)]
        NS32, NS16 = 6, 5
        s32 = [sb("s32_%d" % i, [128, 528], F32) for i in range(NS32)]
        s16 = [sb("s16_%d" % i, [128, 512], BF16) for i in range(NS16)]
        ps_h = st.enter_context(nc.psum_tensor("ps", [128, 8, 512], F32))
        ident_f = sb("ident_f", [128, 128], F32)
        ident_b = sb("ident_b", [128, 128], BF16)
        ones_b = sb("ones_b", [128, 128], BF16)
        cmask = sb("cmask", [128, 128], F32)
        smask = sb("smask", [128, 512], F32)
        invcnt = sb("invcnt", [128, 64], F32)
        sel = sb("sel", [128, 8], F32)
        stg = sb("stg", [128, 128], F32)
        colsT = sb("colsT", [128, 128], F32)
        badaT = sb("badaT", [128, 144], F32)
        modT = sb("modT", [128, 144], F32)
        vecs = sb("vecs", [128, 9, 16], F32)
        cT = sb("cT", [128, 16], BF16)
        cT4 = sb("cT4", [128, 16, 4], BF16)
        selb = sb("selb", [128, 4], F32)
        lbv = sb("lbv", [128, 3, 8], F32)
        epsc = sb("epsc", [128, 1], F32)
        poolw_sb = sb("poolw", [128, 8, 256], BF16)
        S_carry = sb("S_carry", [128, 8, 128], F32)
        halo = sb("halo", [128, 8, 16], F32)
        eb = sb("eb", [128, 1024], F32)
        kt = sb("kt", [128, 1024], BF16)
        kh = sb("kh", [128, 1024], BF16)
        Apair = sb("Apair", [128, 8, 128], BF16)
        kA = sb("kA", [128, 8, 128], BF16)
        kB = sb("kB", [128, 8, 128], BF16)
        Spp = sb("Spp", [128, 2, 128], F32)
        Sbf = sb("Sbf", [128, 16, 128], BF16)
        rstd2 = [sb("rstd%d" % i, [128, 512], F32) for i in range(2)]

        rr = {"ps": 0, "w": 0, "s32": 0, "s16": 0}

        def psb():
            i = rr["ps"]
            rr["ps"] = (i + 1) % 8
            return i, ps_h[:, i, :], ("ps", i)

        def wtile():
            i = rr["w"]
            rr["w"] = (i + 1) % NW
            return wpool[i], ("w", i)

        def sc32():
            i = rr["s32"]
            rr["s32"] = (i + 1) % NS32
            return s32[i], ("s32", i)

        def sc16():
            i = rr["s16"]
            rr["s16"] = (i + 1) % NS16
            return s16[i], ("s16", i)

        flip = {"v": 0}

        def evac_eng():
            flip["v"] ^= 1
            return "act" if flip["v"] else "dve"

        def copy_on(eng, out, in_):
            if eng == "act":
                return lambda e: e.copy(out=out, in_=in_)
            return lambda e: e.tensor_copy(out=out, in_=in_)

        for (t_, d_, k_) in ((ident_f, ident_d, "ident_f"), (cmask, cmask_d, "cmask"), (smask, smask_d, "smask"),
                             (invcnt, invcnt_d, "invcnt"), (sel, sel_d, "sel")):
            p.add("sp", lambda e, t_=t_, d_=d_: e.dma_start(out=t_[:], in_=d_[:, :]), w=[("c", k_)], dma=True)
        p.add("dve", lambda e: e.tensor_copy(out=ident_b[:], in_=ident_f[:]), r=[("c", "ident_f")], w=[("c", "ident_b")])
        p.add("dve", lambda e: e.memset(ones_b[:], 1.0), w=[("c", "ones")])
        p.add("dve", lambda e: e.memset(epsc[:], EPS), w=[("c", "eps")])
        p.add("dve", lambda e: e.memset(stg[:], 0.0), w=[("c", "stg")])
        p.add("dve", lambda e: e.memset(S_carry[:], 0.0), w=[("c", "Sc", h) for h in range(8)])
        p.add("dve", lambda e: e.memset(halo[:], 0.0), w=[("c", "halo", c) for c in range(8)])
        p.add("dve", lambda e: e.memset(kA[:], 0.0), w=[("m", "kA")])
        p.add("dve", lambda e: e.memset(kB[:], 0.0), w=[("m", "kB")])
        p.add("pool", lambda e: e.dma_start(out=poolw_sb[:], in_=pool_w.rearrange("g (kc p) n -> p (g kc) n", p=128)),
              w=[("c", "poolw")], dma=True)
        rows = [(nrm_d.rearrange("a b n -> (a b) n"), 0, 64), (pscale_d, 64, 8), (lb_d, 72, 16), (gn_d, 88, 1),
                (c_d, 89, 16)]
        for (src, r0, n) in rows:
            p.add("sp", lambda e, src=src, r0=r0, n=n: e.dma_start(out=stg[r0:r0 + n, :], in_=src[:, :]),
                  r=[("c", "stg")], w=[("c", "stgr", r0)], dma=True)
        bi, bank, bk = psb()
        p.add("pe", lambda e, bank=bank: e.transpose(out=bank[:, 0:128], in_=stg[:], identity=ident_f[:]),
              r=[("c", "stgr", r0) for (_, r0, _) in rows] + [("c", "ident_f")], w=[bk])
        p.add("dve", lambda e, bank=bank: e.tensor_copy(out=colsT[:], in_=bank[:, 0:128]), r=[bk], w=[("c", "colsT")])
        for (r0, n) in ((0, 128), (128, 16)):
            s_, sk = sc32()
            p.add("sp", lambda e, s_=s_, r0=r0, n=n: e.dma_start(out=s_[0:n, 0:128], in_=b_ada[r0:r0 + n, :]),
                  w=[sk], dma=True)
            bi, bank, bk = psb()
            p.add("pe", lambda e, bank=bank, s_=s_, n=n: e.transpose(out=bank[:, 0:n], in_=s_[0:n, 0:128],
                                                                      identity=ident_f[0:n, 0:n]),
                  r=[sk, ("c", "ident_f")], w=[bk])
            p.add("dve", lambda e, bank=bank, r0=r0, n=n: e.tensor_copy(out=badaT[:, r0:r0 + n], in_=bank[:, 0:n]),
                  r=[bk], w=[("c", "badaT", r0)])
        p.add("act", lambda e: e.activation(out=cT[:], in_=colsT[:, 89:105], func=AF.Silu), r=[("c", "colsT")], w=[("c", "cT")])
        p.add("dve", lambda e: e.tensor_tensor(out=lbv[:, 0, :], in0=colsT[:, 72:80], in1=colsT[:, 80:88], op=ALU.subtract),
              r=[("c", "colsT")], w=[("c", "lbd")])
        p.add("act", lambda e: e.activation(out=lbv[:, 1, :], in_=lbv[:, 0, :], func=AF.Sigmoid), r=[("c", "lbd")], w=[("c", "oml")])
        p.add("dve", lambda e: e.tensor_scalar(out=lbv[:, 2, :], in0=lbv[:, 1, :], scalar1=-1.0, scalar2=None, op0=ALU.mult),
              r=[("c", "oml")], w=[("c", "noml")])
        nw = lambda i: colsT[:, 16 * i:16 * (i + 1)]
        pscale = colsT[:, 64:72]
        gnw = colsT[:, 88:89]

        wada_v = w_ada.rearrange("(kc p) n -> p kc n", p=128) if not exch else None

        def A_(g):
            return vecs[:, 3 * g + 1, :], ("c", "vec", 3 * g + 1)

        def SH_(g):
            return vecs[:, 3 * g, :], ("c", "vec", 3 * g)

        def G_(g):
            return vecs[:, 3 * g + 2, :], ("c", "vec", 3 * g + 2)

        def emit_mod2(seg):
            bi, bank, bk = psb()
            for tt in range(8):
                wt, wk = wtile()
                col0 = seg * D + tt * 256
                wv = wt[:].rearrange("p (kc n) -> p kc n", n=256)
                p.add("pool", lambda e, wv=wv, col0=col0: e.dma_start(out=wv, in_=wada_v[:, :, col0:col0 + 256]), w=[wk], dma=True)
                for jj in range(2):
                    j = tt * 2 + jj
                    for kc in range(KC):
                        p.add("pe", lambda e, bank=bank, wv=wv, jj=jj, kc=kc, j=j: e.matmul(
                            bank[:, j:j + 1], lhsT=wv[:, kc, jj * 128:(jj + 1) * 128], rhs=cT[:, kc:kc + 1],
                            start=(kc == 0), stop=(kc == KC - 1)), r=[wk, ("c", "cT")], w=[bk])
            msl = modT[:, seg * 16:(seg + 1) * 16]
            p.add("dve", lambda e, bank=bank: e.tensor_tensor(out=msl, in0=bank[:, 0:16], in1=badaT[:, seg * 16:(seg + 1) * 16],
                                                              op=ALU.add),
                  r=[bk, ("c", "badaT", 0), ("c", "badaT", 128)], w=[("c", "mod", seg)])
            g = seg // 3
            kind = seg % 3
            dst = vecs[:, seg, :]
            if kind == 0:
                p.add("dve", lambda e: e.tensor_copy(out=dst, in_=msl), r=[("c", "mod", seg)], w=[("c", "vec", seg)])
            elif kind == 1:
                p.add("dve", lambda e: e.scalar_tensor_tensor(out=dst, in0=msl, scalar=1.0, in1=nw(g), op0=ALU.add, op1=ALU.mult),
                      r=[("c", "mod", seg), ("c", "colsT")], w=[("c", "vec", seg)])
            else:
                fac = 1.0 if g == 1 else 0.5
                p.add("dve", lambda e: e.tensor_scalar(out=dst, in0=msl, scalar1=fac, scalar2=None, op0=ALU.mult),
                      r=[("c", "mod", seg)], w=[("c", "vec", seg)])

        def emit_mod_all():
            modrow = slabA.alloc([2304], F32)
            p.add("sp", lambda e: e.dma_start(out=selb[:], in_=selb_d[:, :]), w=[("c", "selb")], dma=True)
            cs_, csk = sc32()
            p.add("sp", lambda e: e.dma_start(out=cs_[0:64, 0:128], in_=call_d[:, :]), w=[csk], dma=True)
            bi, bank, bk = psb()
            p.add("pe", lambda e: e.transpose(out=bank[:, 0:64], in_=cs_[0:64, 0:128], identity=ident_f[0:64, 0:64]),
                  r=[csk, ("c", "ident_f")], w=[bk])
            p.add("act", lambda e: e.activation(out=cT4[:].rearrange("p k b -> p b k"),
                                                in_=bank[:, 0:64].rearrange("p (b k) -> p b k", k=16), func=AF.Silu),
                  r=[bk], w=[("c", "cT4")])
            wsl_v = wsl_d.rearrange("(kc p) n -> p kc n", p=128)
            for tt in range(9):
                wt, wk = wtile()
                wv = wt[:].rearrange("p (kc n) -> p kc n", n=256)
                p.add("pool", lambda e, wv=wv, tt=tt: e.dma_start(out=wv, in_=wsl_v[:, :, tt * 256:(tt + 1) * 256]), w=[wk], dma=True)
                if tt % 2 == 0:
                    bi, bank, bk = psb()
                for kc in range(KC):
                    p.add("pe", lambda e, bank=bank, wv=wv, kc=kc, tt=tt: e.matmul(
                        bank[0:4, (tt % 2) * 256:(tt % 2 + 1) * 256], lhsT=cT4[:, kc, :], rhs=wv[:, kc, :],
                        start=(kc == 0), stop=(kc == KC - 1)), r=[wk, ("c", "cT4")], w=[bk])
                if tt % 2 == 1 or tt == 8:
                    c0 = (tt // 2) * 512
                    n_ = 512 if tt % 2 == 1 else 256
                    p.add("dve", lambda e, bank=bank, c0=c0, n_=n_: e.tensor_copy(out=modrow[0:4, c0:c0 + n_], in_=bank[0:4, 0:n_]),
                          r=[bk], w=[("A", "modrow", tt // 2)])
            p.add("pool", lambda e: e.dma_start(out=cin2_d.rearrange("(jj b) q -> b jj q", b=4),
                                                in_=modrow[0:4, :].rearrange("b (jj q) -> b jj q", q=128)),
                  r=[("A", "modrow", i) for i in range(5)], w=[("dram", "cin2")], dma=True)
            p.add("pool", lambda e: e.collective_compute("AllGather", ALU.bypass, replica_groups=[list(range(8))],
                                                         ins=[cin2_d.opt()], outs=[cout2_d.opt()]),
                  r=[("dram", "cin2")], w=[("dram", "cout2")], dma="cc")
            p.add("dve", lambda e: e.tensor_copy(out=modT[:], in_=badaT[:]), r=[("c", "badaT", 0), ("c", "badaT", 128)],
                  w=[("c", "modT")])
            gview = cout2_d.rearrange("(j b) q -> b j q", b=4)
            for b in range(4):
                for (r0, n) in ((0, 128), (128, 16)):
                    s_, sk = sc32()
                    p.add("pool", lambda e, s_=s_, b=b, r0=r0, n=n: e.dma_start(out=s_[0:n, 0:128], in_=gview[b, r0:r0 + n, :]),
                          r=[("dram", "cout2")], w=[sk], dma=True)
                    bi, bank, bk = psb()
                    p.add("pe", lambda e, bank=bank, s_=s_, n=n: e.transpose(out=bank[:, 0:n], in_=s_[0:n, 0:128],
                                                                              identity=ident_f[0:n, 0:n]),
                          r=[sk, ("c", "ident_f")], w=[bk])
                    p.add("dve", lambda e, bank=bank, b=b, r0=r0, n=n: e.scalar_tensor_tensor(
                        out=modT[:, r0:r0 + n], in0=bank[:, 0:n], scalar=selb[:, b:b + 1], in1=modT[:, r0:r0 + n],
                        op0=ALU.mult, op1=ALU.add), r=[bk, ("c", "selb"), ("c", "modT")], w=[("c", "modT")])
            for seg in range(9):
                g = seg // 3
                kind = seg % 3
                msl = modT[:, seg * 16:(seg + 1) * 16]
                dst = vecs[:, seg, :]
                if kind == 0:
                    p.add("dve", lambda e, dst=dst, msl=msl: e.tensor_copy(out=dst, in_=msl), r=[("c", "modT")], w=[("c", "vec", seg)])
                elif kind == 1:
                    p.add("dve", lambda e, dst=dst, msl=msl, g=g: e.scalar_tensor_tensor(out=dst, in0=msl, scalar=1.0, in1=nw(g),
                                                                                          op0=ALU.add, op1=ALU.mult),
                          r=[("c", "modT"), ("c", "colsT")], w=[("c", "vec", seg)])
                else:
                    fac = 1.0 if g == 1 else 0.5
                    p.add("dve", lambda e, dst=dst, msl=msl, fac=fac: e.tensor_scalar(out=dst, in0=msl, scalar1=fac, scalar2=None,
                                                                                       op0=ALU.mult),
                          r=[("c", "modT")], w=[("c", "vec", seg)])

        def xk(kc, half):
            return ("x", kc, half)

        def hk(kc, half):
            return ("h", kc, half)

        def hs(half):
            return slice(half * 512, (half + 1) * 512)

        def emit_rstd(bank, bk, scale, out_, ok):
            l_, lk = sc32()
            p.add("act", lambda e: e.activation(out=l_[:, 0:512], in_=bank, func=AF.Ln, bias=epsc[:, 0:1], scale=scale),
                  r=[bk, ("c", "eps")], w=[lk])
            p.add("act", lambda e: e.activation(out=out_, in_=l_[:, 0:512], func=AF.Exp, scale=-0.5), r=[lk], w=[ok])

        def emit_norm_mod(g, part=3):
            a_, ak = A_(g)
            sh_, shk = SH_(g)
            for half in range(2):
                if not (part & 1):
                    continue
                bi, bank, bk = psb()
                for kc in range(KC):
                    q_, qk = sc16()
                    p.add("act", lambda e, q_=q_, kc=kc: e.activation(out=q_[:], in_=xT[:, kc, hs(half)], func=AF.Square),
                          r=[xk(kc, half)], w=[qk])
                    p.add("pe", lambda e, q_=q_, kc=kc, bank=bank: e.matmul(bank, lhsT=ones_b[:], rhs=q_[:], start=(kc == 0),
                                                                            stop=(kc == KC - 1)),
                          r=[qk, ("c", "ones")], w=[bk])
                rs_, rk = rstd2[half], ("c", "rstd", half)
                emit_rstd(bank, bk, 1.0 / D, rs_[:, 0:512], rk)
            for half in range(2):
                if not (part & 2):
                    continue
                rs_, rk = rstd2[half], ("c", "rstd", half)
                for kc in range(KC):
                    t_, tk = sc32()
                    p.add("dve", lambda e, t_=t_, kc=kc: e.scalar_tensor_tensor(
                        out=t_[:, 0:512], in0=xT[:, kc, hs(half)], scalar=a_[:, kc:kc + 1], in1=rs_[:, 0:512],
                        op0=ALU.mult, op1=ALU.mult), r=[xk(kc, half), rk, ak], w=[tk])
                    p.add("act", lambda e, t_=t_, kc=kc: e.activation(out=hT[:, kc, hs(half)], in_=t_[:, 0:512], func=AF.Identity,
                                                                      bias=sh_[:, kc:kc + 1]),
                          r=[tk, shk], w=[hk(kc, half)])

        def emit_ffn(fi, g):
            wg_d, wu_d, wd_d = ffn_d[fi]
            wg_v = wg_d.rearrange("(kc p) n -> p kc n", p=128)
            wu_v = wu_d.rearrange("(kc p) n -> p kc n", p=128)
            wd_v = wd_d.rearrange("(c p) n -> p c n", p=128)
            hg_, hgk = G_(g)
            act = slabA_h[:, :].bitcast(BF16).rearrange("p (c t) -> p c t", t=T)
            for (c0, nch) in FF_GROUPS:
                for cp in range(nch // 2):
                    wgt, wgk = wtile()
                    wut, wuk = wtile()
                    col0 = (c0 + 2 * cp) * 128
                    wgv = wgt[:].rearrange("p (kc n) -> p kc n", n=256)
                    wuv = wut[:].rearrange("p (kc n) -> p kc n", n=256)
                    p.add("pool", lambda e, wgv=wgv, col0=col0: e.dma_start(out=wgv, in_=wg_v[:, :, col0:col0 + 256]), w=[wgk], dma=True)
                    p.add("pool", lambda e, wuv=wuv, col0=col0: e.dma_start(out=wuv, in_=wu_v[:, :, col0:col0 + 256]), w=[wuk], dma=True)
                    sil = {}
                    for cc in range(2):
                        for half in range(2):
                            gi_, gbank, gk = psb()
                            for kc in range(KC):
                                p.add("pe", lambda e, gbank=gbank, wgv=wgv, cc=cc, kc=kc, half=half: e.matmul(
                                    gbank, lhsT=wgv[:, kc, cc * 128:(cc + 1) * 128], rhs=hT[:, kc, hs(half)],
                                    start=(kc == 0), stop=(kc == KC - 1)), r=[wgk, hk(kc, half)], w=[gk])
                            s_, sk = sc16()
                            p.add("act", lambda e, s_=s_, gbank=gbank: e.activation(out=s_[:], in_=gbank, func=AF.Silu), r=[gk], w=[sk])
                            sil[(cc, half)] = (s_, sk)
                    for cc in range(2):
                        cl = 2 * cp + cc
                        for half in range(2):
                            ui_, ubank, uk = psb()
                            for kc in range(KC):
                                p.add("pe", lambda e, ubank=ubank, wuv=wuv, cc=cc, kc=kc, half=half: e.matmul(
                                    ubank, lhsT=wuv[:, kc, cc * 128:(cc + 1) * 128], rhs=hT[:, kc, hs(half)],
                                    start=(kc == 0), stop=(kc == KC - 1)), r=[wuk, hk(kc, half)], w=[uk])
                            s_, sk = sil[(cc, half)]
                            p.add("dve", lambda e, s_=s_, ubank=ubank, cl=cl, half=half: e.tensor_tensor(
                                out=act[:, cl, hs(half)], in0=s_[:], in1=ubank, op=ALU.mult), r=[sk, uk], w=[("A", cl, half)])
                for dp in range(8):
                    wdt, wdk = wtile()
                    wdv = wdt[:, 0:nch * 256].rearrange("p (c n) -> p c n", n=256)
                    p.add("pool", lambda e, wdv=wdv, dp=dp, c0=c0, nch=nch: e.dma_start(
                        out=wdv, in_=wd_v[:, c0:c0 + nch, dp * 256:(dp + 1) * 256]), w=[wdk], dma=True)
                    for dc in range(2):
                        dch = dp * 2 + dc
                        for half in range(2):
                            bi_, bank, bk = psb()
                            for j in range(nch):
                                p.add("pe", lambda e, bank=bank, wdv=wdv, dc=dc, j=j, half=half: e.matmul(
                                    bank, lhsT=wdv[:, j, dc * 128:(dc + 1) * 128], rhs=act[:, j, hs(half)],
                                    start=(j == 0), stop=(j == nch - 1)), r=[wdk, ("A", j, half)], w=[bk])
                            p.add("dve", lambda e, bank=bank, dch=dch, half=half: e.scalar_tensor_tensor(
                                out=xT[:, dch, hs(half)], in0=bank, scalar=hg_[:, dch:dch + 1], in1=xT[:, dch, hs(half)],
                                op0=ALU.mult, op1=ALU.add), r=[bk, hgk, xk(dch, half)], w=[xk(dch, half)])

        def emit_load_x(blk):
            slabA.reset()
            stgx = [slabA.alloc([2048], F32) for _ in range(2)]
            for j in range(8):
                sx = stgx[j % 2]
                skey = ("A", "stgx", j % 2)
                r0 = blk * T + j * 128
                p.add("sp", lambda e, sx=sx, r0=r0: e.dma_start(out=sx, in_=x_d[r0:r0 + 128, :]), w=[skey], dma=True)
                for q4 in range(4):
                    bi_, bank, bk = psb()
                    for i in range(4):
                        kc = q4 * 4 + i
                        p.add("pe", lambda e, bank=bank, sx=sx, kc=kc, i=i: e.transpose(
                            out=bank[:, i * 128:(i + 1) * 128], in_=sx[:, kc * 128:(kc + 1) * 128], identity=ident_f[:]),
                            r=[skey, ("c", "ident_f")], w=[bk])
                    eng = evac_eng()
                    p.add(eng, copy_on(eng, xT[:, q4 * 4:q4 * 4 + 4, j * 128:(j + 1) * 128],
                                       bank.rearrange("p (a b) -> p a b", b=128)),
                          r=[bk], w=[xk(q4 * 4 + i, j // 4) for i in range(4)])

        def emit_final(blk):
            slabA.reset()
            stgo = [slabA.alloc([2048], F32) for _ in range(2)]
            fw = nw(3)
            rs = []
            for half in range(2):
                bi, bank, bk = psb()
                for kc in range(KC):
                    q_, qk = sc16()
                    p.add("act", lambda e, q_=q_, kc=kc, half=half: e.activation(out=q_[:], in_=xT[:, kc, hs(half)], func=AF.Square),
                          r=[xk(kc, half)], w=[qk])
                    p.add("pe", lambda e, q_=q_, kc=kc, bank=bank: e.matmul(bank, lhsT=ones_b[:], rhs=q_[:], start=(kc == 0),
                                                                            stop=(kc == KC - 1)), r=[qk, ("c", "ones")], w=[bk])
                rs_, rk = sc32()
                emit_rstd(bank, bk, 1.0 / D, rs_[:, 0:512], rk)
                for kc in range(KC):
                    p.add("dve", lambda e, kc=kc, half=half, rs_=rs_: e.scalar_tensor_tensor(
                        out=xT[:, kc, hs(half)], in0=xT[:, kc, hs(half)], scalar=fw[:, kc:kc + 1], in1=rs_[:, 0:512],
                        op0=ALU.mult, op1=ALU.mult), r=[xk(kc, half), rk, ("c", "colsT")], w=[xk(kc, half)])
            for j in range(8):
                so = stgo[j % 2]
                okey = ("A", "stgo", j % 2)
                for q4 in range(4):
                    bi_, bank, bk = psb()
                    for i in range(4):
                        kc = q4 * 4 + i
                        p.add("pe", lambda e, bank=bank, kc=kc, i=i, j=j: e.transpose(
                            out=bank[:, i * 128:(i + 1) * 128], in_=xT[:, kc, j * 128:(j + 1) * 128], identity=ident_f[:]),
                            r=[xk(kc, j // 4), ("c", "ident_f")], w=[bk])
                    eng = evac_eng()
                    p.add(eng, copy_on(eng, so[:, q4 * 512:(q4 + 1) * 512], bank), r=[bk], w=[okey])
                r0 = blk * T + j * 128
                p.add("sp", lambda e, so=so, r0=r0: e.dma_start(out=out_d[r0:r0 + 128, :], in_=so), r=[okey], w=[("o", blk, j)], dma=True)

        win_v = w_in.rearrange("(kc p) n -> p kc n", p=128)
        wout_v = w_out.rearrange("(kc p) n -> p kc n", p=128)

        def emit_wout(ymix, rowhalf, ykeyf):
            g2_, g2k = G_(1)
            for dp in range(8):
                wt, wk = wtile()
                wv = wt[:, 0:2048].rearrange("p (c n) -> p c n", n=256)
                p.add("pool", lambda e, wv=wv, dp=dp: e.dma_start(
                    out=wv, in_=wout_v[:, rowhalf * 8:rowhalf * 8 + 8, dp * 256:(dp + 1) * 256]), w=[wk], dma=True)
                for dc in range(2):
                    dch = dp * 2 + dc
                    for half in range(2):
                        bi_, bank, bk = psb()
                        for j in range(8):
                            p.add("pe", lambda e, bank=bank, wv=wv, dc=dc, j=j, half=half: e.matmul(
                                bank, lhsT=wv[:, j, dc * 128:(dc + 1) * 128], rhs=ymix[:, j, hs(half)],
                                start=(j == 0), stop=(j == 7)), r=[wk, ykeyf(j, half)], w=[bk])
                        p.add("dve", lambda e, bank=bank, dch=dch, half=half: e.scalar_tensor_tensor(
                            out=xT[:, dch, hs(half)], in0=bank, scalar=g2_[:, dch:dch + 1], in1=xT[:, dch, hs(half)],
                            op0=ALU.mult, op1=ALU.add), r=[bk, g2k, xk(dch, half)], w=[xk(dch, half)])

        def emit_mixer(blk, state=False):
            slabA.reset()
            ymix = slabA.alloc([8, T], BF16)
            qt2 = [slabA.alloc([T], BF16) for _ in range(2)]
            vt2 = [slabA.alloc([8, 128], BF16) for _ in range(2)]
            sgz2 = [slabA.alloc([T], BF16) for _ in range(2)]
            pooled = slabA.alloc([2, T], BF16)
            if state:
                gsc = ymix[:, 0:3, :].bitcast(F32).rearrange("p a (b c) -> p (a b) c", c=128)[:, 0:9, :]
                halo_new = pooled[:, 0, 0:256].bitcast(F32).rearrange("p (c t) -> p c t", t=16)

            def yk(c, half):
                return ("A", "y", c, half)

            if state:
                for gi in range(4):
                    wt, wk = wtile()
                    wv = wt[:].rearrange("p (kc n) -> p kc n", n=256)
                    p.add("pool", lambda e, wv=wv, gi=gi: e.dma_start(out=wv, in_=win_v[:, :, gi * 256:(gi + 1) * 256]), w=[wk], dma=True)
                    bi_, bank, bk = psb()
                    for cc in range(2):
                        for kc in range(KC):
                            p.add("pe", lambda e, bank=bank, wv=wv, cc=cc, kc=kc: e.matmul(
                                bank[:, cc * 16:(cc + 1) * 16], lhsT=wv[:, kc, cc * 128:(cc + 1) * 128], rhs=hT[:, kc, T - 16:T],
                                start=(kc == 0), stop=(kc == KC - 1)), r=[wk, hk(kc, 1)], w=[bk])
                    p.add("dve", lambda e, bank=bank, gi=gi: e.tensor_copy(
                        out=halo_new[:, 2 * gi:2 * gi + 2, :], in_=bank[:, 0:32].rearrange("p (a b) -> p a b", b=16)),
                        r=[bk], w=[("A", "halo_new", gi)])

            def emit_pool_branch():
              for gi in range(4):
                  w = WIN[gi]
                  nlev = gi + 1
                  wt, wk = wtile()
                  wv = wt[:].rearrange("p (kc n) -> p kc n", n=256)
                  p.add("pool", lambda e, wv=wv, gi=gi: e.dma_start(out=wv, in_=win_v[:, :, gi * 256:(gi + 1) * 256]), w=[wk], dma=True)
                  for cc in range(2):
                      c = 2 * gi + cc
                      for half in range(2):
                          bi_, bank, bk = psb()
                          for kc in range(KC):
                              p.add("pe", lambda e, bank=bank, wv=wv, cc=cc, kc=kc, half=half: e.matmul(
                                  bank, lhsT=wv[:, kc, cc * 128:(cc + 1) * 128], rhs=hT[:, kc, hs(half)],
                                  start=(kc == 0), stop=(kc == KC - 1)), r=[wk, hk(kc, half)], w=[bk])
                          u_, uk = sc32()
                          p.add("act", lambda e, u_=u_, bank=bank: e.copy(out=u_[:, 16:528], in_=bank), r=[bk], w=[uk])
                          p.add("dve", lambda e, u_=u_, c=c: e.tensor_copy(out=u_[:, 0:16], in_=halo[:, c, :]),
                                r=[("c", "halo", c), uk], w=[uk])
                          p.add("dve", lambda e, u_=u_, c=c: e.tensor_copy(out=halo[:, c, :], in_=u_[:, 512:528]),
                                r=[uk], w=[("c", "halo", c)])
                          cur, ck = u_, uk
                          sh = 1
                          for lv in range(nlev):
                              n_, nk = sc32()
                              lo = 2 * sh - 1
                              p.add("dve", lambda e, n_=n_, cur=cur, lo=lo, sh=sh: e.tensor_tensor(
                                  out=n_[:, lo:528], in0=cur[:, lo:528], in1=cur[:, lo - sh:528 - sh], op=ALU.add),
                                  r=[ck], w=[nk])
                              cur, ck = n_, nk
                              sh *= 2
                          p.add("dve", lambda e, cur=cur, u_=u_, cc=cc, half=half, w=w: e.scalar_tensor_tensor(
                              out=pooled[:, cc, hs(half)], in0=cur[:, 16:528], scalar=1.0 / w, in1=u_[:, 16:528],
                              op0=ALU.mult, op1=ALU.subtract), r=[ck, uk], w=[("A", "pl", cc, half)])
                          if blk == 0 and half == 0:
                              t_, tk = sc32()
                              p.add("dve", lambda e, t_=t_, cur=cur, gi=gi: e.tensor_tensor(
                                  out=t_[:, 0:16], in0=cur[:, 16:32], in1=invcnt[:, gi * 16:(gi + 1) * 16], op=ALU.mult),
                                  r=[ck, ("c", "invcnt")], w=[tk])
                              p.add("dve", lambda e, t_=t_, u_=u_, cc=cc: e.tensor_tensor(
                                  out=pooled[:, cc, 0:16], in0=t_[:, 0:16], in1=u_[:, 16:32], op=ALU.subtract),
                                  r=[tk, uk], w=[("A", "pl", cc, 0)])
                  for dc in range(2):
                      co = 2 * gi + dc
                      for half in range(2):
                          bi_, bank, bk = psb()
                          for kc in range(2):
                              p.add("pe", lambda e, bank=bank, gi=gi, kc=kc, dc=dc, half=half: e.matmul(
                                  bank, lhsT=poolw_sb[:, gi * 2 + kc, dc * 128:(dc + 1) * 128], rhs=pooled[:, kc, hs(half)],
                                  start=(kc == 0), stop=(kc == 1)), r=[("c", "poolw"), ("A", "pl", kc, half)], w=[bk])
                          p.add("act", lambda e, bank=bank, co=co, half=half: e.activation(
                              out=ymix[:, co, hs(half)], in_=bank, func=AF.Identity, scale=pscale[:, co:co + 1]),
                              r=[bk, ("c", "colsT")], w=[yk(co, half)])

            if stop == 31:
                emit_pool_branch()
                emit_wout(ymix, 0, yk)
                return

            def wtile128(col0):
                wt, wk = wtile()
                wv = wt[:, 0:2048].rearrange("p (kc n) -> p kc n", n=128)
                p.add("pool", lambda e: e.dma_start(out=wv, in_=win_v[:, :, col0:col0 + 128]), w=[wk], dma=True)
                return wv, wk

            def proj_fm(wv, wk, half):
                bi_, bank, bk = psb()
                for kc in range(KC):
                    p.add("pe", lambda e, kc=kc: e.matmul(bank, lhsT=wv[:, kc, :], rhs=hT[:, kc, hs(half)],
                                                          start=(kc == 0), stop=(kc == KC - 1)), r=[wk, hk(kc, half)], w=[bk])
                return bank, bk

            def emit_proj(h):
                db = h % 2
                qt, vt, sgz = qt2[db], vt2[db], sgz2[db]
                oml = lbv[:, 1, h:h + 1]
                noml = lbv[:, 2, h:h + 1]
                wv, wk = wtile128(3072 + h * 128)
                vis = []
                for half in range(2):
                    bank, bk = proj_fm(wv, wk, half)
                    vi_, vik = sc16()
                    eng = evac_eng()
                    p.add(eng, copy_on(eng, vi_[:], bank), r=[bk], w=[vik])
                    vis.append((vi_, vik))
                wv, wk = wtile128(2048 + h * 128)
                for half in range(2):
                    bank, bk = proj_fm(wv, wk, half)
                    sg_, sgk = sc32()
                    p.add("act", lambda e, sg_=sg_, bank=bank: e.activation(out=sg_[:, 0:512], in_=bank, func=AF.Sigmoid, scale=-1.0),
                          r=[bk], w=[sgk])
                    gb_, gbk = sc32()
                    p.add("act", lambda e, sg_=sg_, gb_=gb_: e.activation(out=gb_[:, 0:512], in_=sg_[:, 0:512], func=AF.Ln,
                                                                          bias=1.0, scale=noml), r=[sgk, ("c", "noml")], w=[gbk])
                    b_, bk2 = sc32()
                    p.add("dve", lambda e, gb_=gb_, b_=b_: e.tensor_tensor_scan(
                        out=b_[:, 0:512], data0=smask[:], data1=gb_[:, 0:512], initial=0.0, op0=ALU.mult, op1=ALU.add),
                        r=[gbk, ("c", "smask")], w=[bk2])
                    en_, enk = sc32()
                    p.add("act", lambda e, b_=b_, en_=en_: e.activation(out=en_[:, 0:512], in_=b_[:, 0:512], func=AF.Exp, scale=-1.0),
                          r=[bk2], w=[enk])
                    p.add("dve", lambda e, sg_=sg_, en_=en_, half=half: e.scalar_tensor_tensor(
                        out=kt[:, hs(half)], in0=sg_[:, 0:512], scalar=oml, in1=en_[:, 0:512], op0=ALU.mult, op1=ALU.mult),
                        r=[sgk, enk, ("c", "oml")], w=[("m", "kt", half)])
                    p.add("act", lambda e, b_=b_, half=half: e.activation(out=eb[:, hs(half)], in_=b_[:, 0:512], func=AF.Exp),
                          r=[bk2], w=[("m", "eb", half)])
                    ebv = eb[:, hs(half)].rearrange("p (a b) -> p a b", b=64)
                    p.add("dve", lambda e, half=half, ebv=ebv: e.tensor_tensor(
                        out=kh[:, hs(half)].rearrange("p (a b) -> p a b", b=64),
                        in0=kt[:, hs(half)].rearrange("p (a b) -> p a b", b=64),
                        in1=ebv[:, :, 63:64].to_broadcast([128, 8, 64]), op=ALU.mult),
                        r=[("m", "kt", half), ("m", "eb", half)], w=[("m", "kh", half)])
                wv, wk = wtile128(1024 + h * 128) if not state else (None, None)
                for half in (range(2) if not state else ()):
                    bank, bk = proj_fm(wv, wk, half)
                    sq_, sqk = sc32()
                    p.add("act", lambda e, sq_=sq_, bank=bank: e.activation(out=sq_[:, 0:512], in_=bank, func=AF.Silu), r=[bk], w=[sqk])
                    p.add("dve", lambda e, sq_=sq_, half=half: e.tensor_tensor(out=qt[:, hs(half)], in0=sq_[:, 0:512],
                                                                               in1=eb[:, hs(half)], op=ALU.mult),
                          r=[sqk, ("m", "eb", half)], w=[("A", "qt", db, half)])
                wv, wk = wtile128(4096 + h * 128) if not state else (None, None)
                for half in (range(2) if not state else ()):
                    bank, bk = proj_fm(wv, wk, half)
                    p.add("act", lambda e, bank=bank, half=half: e.activation(out=sgz[:, hs(half)], in_=bank, func=AF.Silu),
                          r=[bk], w=[("A", "sgz", db, half)])
                bi_, bank, bk = psb()
                bankb = bank.bitcast(BF16).rearrange("p (a b) -> p a b", b=128)
                for j in range(8):
                    vi_, vik = vis[j // 4]
                    p.add("pe", lambda e, j=j, vi_=vi_: e.transpose(out=bankb[:, j, :], in_=vi_[:, (j % 4) * 128:(j % 4 + 1) * 128],
                                                                    identity=ident_b[:]),
                          r=[vik, ("c", "ident_b")], w=[bk])
                eng = evac_eng()
                p.add(eng, copy_on(eng, vt[:, :, :], bankb), r=[bk], w=[("A", "vt", db, 0), ("A", "vt", db, 1)])

            sadd = {}

            def emit_s12(h):
                db = h % 2
                qt, vt = qt2[db], vt2[db]
                bi_, bank, bk = psb()
                bankb = bank.bitcast(BF16).rearrange("p (a b) -> p a b", b=128)
                for j in range(8):
                    p.add("pe", lambda e, j=j: e.transpose(out=bankb[:, j, :], in_=kh[:, j * 128:(j + 1) * 128], identity=ident_b[:]),
                          r=[("m", "kh", j // 4), ("c", "ident_b")], w=[bk])
                p.add("dve", lambda e: e.tensor_copy(out=kA[0:64, :, :], in_=bankb[0:64, :, :]), r=[bk], w=[("m", "kA")])
                p.add("dve", lambda e: e.tensor_copy(out=kB[64:128, :, :], in_=bankb[64:128, :, :]), r=[bk], w=[("m", "kB")])
                for jh in (range(2) if not state else ()):
                    bi_, bank, bk = psb()
                    for jj in range(4):
                        j = jh * 4 + jj
                        p.add("pe", lambda e, bank=bank, jj=jj, j=j: e.matmul(
                            bank[:, jj * 128:(jj + 1) * 128], lhsT=kt[:, j * 128:(j + 1) * 128], rhs=qt[:, j * 128:(j + 1) * 128],
                            start=True, stop=True), r=[("m", "kt", jh), ("A", "qt", db, jh)], w=[bk])
                    p.add("dve", lambda e, bank=bank, jh=jh: e.tensor_tensor(
                        out=Apair[:, jh * 4:jh * 4 + 4, :], in0=bank.rearrange("p (a b) -> p a b", b=128),
                        in1=cmask[:].rearrange("p (a b) -> p a b", a=1).to_broadcast([128, 4, 128]), op=ALU.mult),
                        r=[bk, ("c", "cmask")], w=[("m", "Ap", jh)])
                for q4 in range(4):
                    bi_, bank2, bk2 = psb()
                    for i in range(4):
                        n = q4 * 4 + i
                        j = n // 2
                        src = kA if n % 2 == 0 else kB
                        srk = ("m", "kA") if n % 2 == 0 else ("m", "kB")
                        p.add("pe", lambda e, bank2=bank2, i=i, j=j, src=src: e.matmul(
                            bank2[:, i * 128:(i + 1) * 128], lhsT=src[:, j, :], rhs=vt[:, j, :], start=True, stop=True),
                            r=[srk, ("A", "vt", db, j // 4)], w=[bk2])
                    sadd[q4] = (bank2, bk2)

            def emit_chain(h):
                prev = S_carry[:, h, :]
                prevk = ("c", "Sc", h)
                for n in range(16):
                    bank2, bk2 = sadd[n // 4]
                    if not state:
                        p.add("act", lambda e, prev=prev, n=n: e.copy(out=Sbf[:, n, :], in_=prev), r=[prevk], w=[("m", "Sbf", n)])
                    if n == 15:
                        nxt, nxk = S_carry[:, h, :], ("c", "Sc", h)
                    else:
                        nxt, nxk = Spp[:, n % 2, :], ("m", "Spp", n % 2)
                    col = n * 64 + 63
                    p.add("dve", lambda e, prev=prev, nxt=nxt, n=n, bank2=bank2, col=col: e.scalar_tensor_tensor(
                        out=nxt, in0=prev, scalar=eb[:, col:col + 1], in1=bank2[:, (n % 4) * 128:(n % 4 + 1) * 128],
                        op0=ALU.mult, op1=ALU.add), r=[prevk, ("m", "eb", n // 8), bk2], w=[nxk])
                    prev, prevk = nxt, nxk

            def emit_s45(h):
                db = h % 2
                qt, vt, sgz = qt2[db], vt2[db], sgz2[db]
                st1 = []
                for jh in range(2):
                    bi_, bank, bk = psb()
                    for jj in range(4):
                        j = jh * 4 + jj
                        o_ = bank[:, jj * 128:(jj + 1) * 128]
                        p.add("pe", lambda e, o_=o_, j=j: e.matmul(o_, lhsT=vt[:, j, :], rhs=Apair[:, j, :], start=True, stop=False),
                              r=[("A", "vt", db, jh), ("m", "Ap", jh)], w=[bk])
                        for i in range(2):
                            n = 2 * j + i
                            p.add("pe", lambda e, o_=o_, n=n, i=i: e.matmul(
                                o_[:, i * 64:(i + 1) * 64], lhsT=Sbf[:, n, :], rhs=qt[:, n * 64:(n + 1) * 64],
                                start=False, stop=(i == 1)), r=[("m", "Sbf", n), ("A", "qt", db, jh)], w=[bk])
                    o32, ok = sc32()
                    p.add("dve", lambda e, o32=o32, bank=bank: e.tensor_copy(out=o32[:, 0:512], in_=bank), r=[bk], w=[ok])
                    q_, qk = sc16()
                    p.add("act", lambda e, q_=q_, o32=o32: e.activation(out=q_[:], in_=o32[:, 0:512], func=AF.Square), r=[ok], w=[qk])
                    st1.append((o32, ok, q_, qk))
                for jh in range(2):
                    o32, ok, q_, qk = st1[jh]
                    bi_, b2, bk2 = psb()
                    p.add("pe", lambda e, b2=b2, q_=q_: e.matmul(b2, lhsT=ones_b[:], rhs=q_[:], start=True, stop=True),
                          r=[qk, ("c", "ones")], w=[bk2])
                    rs_, rk = rstd2[jh], ("c", "rstd", jh)
                    emit_rstd(b2, bk2, 1.0 / 128, rs_[:, 0:512], rk)
                    t_, tk = sc32()
                    p.add("dve", lambda e, t_=t_, o32=o32, rs_=rs_: e.scalar_tensor_tensor(
                        out=t_[:, 0:512], in0=o32[:, 0:512], scalar=gnw, in1=rs_[:, 0:512], op0=ALU.mult, op1=ALU.mult),
                        r=[ok, rk, ("c", "colsT")], w=[tk])
                    p.add("dve", lambda e, t_=t_, jh=jh: e.tensor_tensor(out=ymix[:, h, hs(jh)], in0=t_[:, 0:512],
                                                                         in1=sgz[:, hs(jh)], op=ALU.mult),
                          r=[tk, ("A", "sgz", db, jh)], w=[yk(h, jh)])

            if state:
                for h in range(8):
                    emit_proj(h)
                    emit_s12(h)
                    emit_chain(h)
                p.add("pool", lambda e: e.dma_start(out=cin_d[0:1024, :].rearrange("(h q) n -> q h n", q=128), in_=S_carry[:]),
                      r=[("c", "Sc", h) for h in range(8)], w=[("dram", "cin", 0)], dma=True)
                p.add("pool", lambda e: e.dma_start(out=cin_d[1024:1152, :], in_=halo_new.rearrange("p c t -> p (c t)")),
                      r=[("A", "halo_new", gi) for gi in range(4)], w=[("dram", "cin", 1)], dma=True)
                p.add("pool", lambda e: e.collective_compute("AllGather", ALU.bypass, replica_groups=[list(range(8))],
                                                             ins=[cin_d.opt()], outs=[cout_d.opt()]),
                      r=[("dram", "cin", 0), ("dram", "cin", 1)], w=[("dram", "cout")], dma="cc")
                for ri, r_ in enumerate((0, 2, 4, 6)):
                    base = r_ * 1152
                    p.add("pool", lambda e, base=base: e.dma_start(
                        out=gsc[:, 0:8, :], in_=cout_d[base:base + 1024, :].rearrange("(h q) n -> q h n", q=128)),
                        r=[("dram", "cout")], w=[("A", "gsc", 0)], dma=True)
                    p.add("pool", lambda e, base=base: e.dma_start(out=gsc[:, 8, :], in_=cout_d[base + 1024:base + 1152, :]),
                          r=[("dram", "cout")], w=[("A", "gsc", 1)], dma=True)
                    Sall = S_carry[:].rearrange("p h n -> p (h n)")
                    hall = halo[:].rearrange("p c t -> p (c t)")
                    gs_ = gsc[:, 0:8, :].rearrange("p h n -> p (h n)")
                    skeys = [("c", "Sc", h) for h in range(8)]
                    hkeys = [("c", "halo", c) for c in range(8)]
                    if ri == 0:
                        p.add("dve", lambda e, r_=r_: e.tensor_scalar(out=Sall, in0=gs_, scalar1=sel[:, r_:r_ + 1], scalar2=None, op0=ALU.mult),
                              r=[("A", "gsc", 0), ("c", "sel"), ("dram", "cin", 0)], w=skeys)
                        p.add("dve", lambda e, r_=r_: e.tensor_scalar(out=hall, in0=gsc[:, 8, :], scalar1=sel[:, r_:r_ + 1], scalar2=None, op0=ALU.mult),
                              r=[("A", "gsc", 1), ("c", "sel")], w=hkeys)
                    else:
                        p.add("dve", lambda e, r_=r_: e.scalar_tensor_tensor(out=Sall, in0=gs_, scalar=sel[:, r_:r_ + 1], in1=Sall,
                                                                             op0=ALU.mult, op1=ALU.add),
                              r=[("A", "gsc", 0), ("c", "sel")] + skeys, w=skeys)
                        p.add("dve", lambda e, r_=r_: e.scalar_tensor_tensor(out=hall, in0=gsc[:, 8, :], scalar=sel[:, r_:r_ + 1], in1=hall,
                                                                             op0=ALU.mult, op1=ALU.add),
                              r=[("A", "gsc", 1), ("c", "sel")] + hkeys, w=hkeys)
                return
            emit_proj(0)
            if stop == 32:
                return
            for h in range(8):
                emit_s12(h)
                if stop == 33:
                    return
                emit_chain(h)
                if stop == 34:
                    return
                if h + 1 < 8:
                    emit_proj(h + 1)
                emit_s45(h)
                if stop == 35 or stop == 40 + h:
                    return
            emit_wout(ymix, 1, yk)
            emit_pool_branch()
            emit_wout(ymix, 0, yk)

        def dump_x():
            for kc in range(KC):
                p.add("sp", lambda e, kc=kc: e.dma_start(out=dbg_d[:, kc, :], in_=xT[:, kc, :]),
                      r=[xk(kc, 0), xk(kc, 1)], w=[("o", "dbg", kc)], dma=True)

        def run_all():
            for blk in range(nblk):
                p.switch("A")
                emit_load_x(blk)
                if exch:
                    emit_norm_mod(0, part=1)
                    emit_mod_all()
                if stop == 1:
                    return
                if blk == 0 and not exch:
                    for seg in (0, 1, 2):
                        emit_mod2(seg)
                if stop == 11:
                    p.add("dve", lambda e: e.tensor_copy(out=xT[:, 0, 0:144], in_=vecs[:].rearrange("p a b -> p (a b)")),
                          r=[("c", "vec", s_) for s_ in range(3)], w=[xk(0, 0)])
                    p.add("dve", lambda e: e.tensor_copy(out=xT[:, 1, 0:144], in_=modT[:]),
                          r=[("c", "mod", s_) for s_ in range(3)], w=[xk(1, 0)])
                    p.add("dve", lambda e: e.tensor_copy(out=xT[:, 2, 0:128], in_=colsT[:]), r=[("c", "colsT")], w=[xk(2, 0)])
                    p.add("dve", lambda e: e.tensor_copy(out=xT[:, 3, 0:144], in_=badaT[:]), r=[("c", "badaT", 0), ("c", "badaT", 128)], w=[xk(3, 0)])
                    p.add("dve", lambda e: e.tensor_copy(out=xT[:, 4, 0:16], in_=cT[:]), r=[("c", "cT")], w=[xk(4, 0)])
                    return
                emit_norm_mod(0, part=(2 if exch else 3))
                if stop == 12:
                    for kc_ in range(KC):
                        for hf_ in range(2):
                            p.add("act", lambda e: e.copy(out=xT[:, kc_, hs(hf_)], in_=hT[:, kc_, hs(hf_)]),
                                  r=[hk(kc_, hf_)], w=[xk(kc_, hf_)])
                    return
                p.switch("A")
                emit_ffn(0, 0)
                if stop == 2:
                    return
                if blk == 0 and not exch:
                    for seg in (3, 4, 5):
                        emit_mod2(seg)
                emit_norm_mod(1)
                if exch:
                    p.switch("A")
                    emit_mixer(blk, state=True)
                p.switch("A")
                emit_mixer(blk)
                if stop == 3 or 30 < stop < 50:
                    return
                if blk == 0 and not exch:
                    for seg in (6, 7, 8):
                        emit_mod2(seg)
                emit_norm_mod(2)
                p.switch("A")
                emit_ffn(1, 2)
                if stop == 4:
                    return
                p.switch("A")
                emit_final(blk)

        run_all()
        if dbg:
            dump_x()
        okeys = [k for k in p.last_w if k[0] == "o"]
        p.add("sp", None, r=okeys)
        p.emit(nc, st)
    return nc, p


def _consts():
    ident = np.eye(128, dtype=np.float32)
    s = np.arange(128)[:, None]
    t = np.arange(128)[None, :]
    cmask = ((s // 64 == t // 64) & (s <= t)).astype(np.float32)
    smask = np.ones((128, 512), np.float32)
    smask[:, ::64] = 0.0
    invcnt = np.zeros((128, 64), np.float32)
    for gi, w in enumerate(WIN):
        invcnt[:, gi * 16:(gi + 1) * 16] = 1.0 / np.minimum(np.arange(16) + 1, w).astype(np.float32)
    return ident, cmask, smask, invcnt


_CACHE = {}


def make_in_maps8(inp):
    maps4 = make_in_maps(inp, 2, 4)
    maps = []
    for i in range(8):
        b, hh = i // 2, i % 2
        m = dict(maps4[b])
        m["x"] = np.ascontiguousarray(maps4[b]["x"][hh * T:(hh + 1) * T])
        sel = np.zeros((128, 8), np.float32)
        if hh == 1:
            sel[:, i - 1] = 1.0
            ic = np.zeros((128, 64), np.float32)
            for gi, w in enumerate(WIN):
                ic[:, gi * 16:(gi + 1) * 16] = 1.0 / w
            m["invcnt"] = ic
        m["sel"] = sel
        del m["w_ada"]
        m["w_ada_sl"] = np.ascontiguousarray(maps4[0]["w_ada"][:, i * 2304:(i + 1) * 2304])
        m["c_all"] = np.ascontiguousarray(np.asarray(inp["c"], dtype=np.float32).reshape(64, 128))
        sb_ = np.zeros((128, 4), np.float32)
        sb_[:, b] = 1.0
        m["selb"] = sb_
        maps.append(m)
    return maps


def make_in_maps(inp, nblk=2, ncores=4):
    ident, cmask, smask, invcnt = _consts()
    f = lambda a: np.ascontiguousarray(np.asarray(a, dtype=np.float32))
    shared = {
        "w_ada": f(inp["w_ada"][0]),
        "b_ada": f(inp["b_ada"][0]).reshape(144, 128),
        "nrm": f(np.stack([inp["norm1_w"][0], inp["norm2_w"][0], inp["norm3_w"][0], inp["final_norm_w"]])).reshape(4, 16, 128),
        "ffn1_gate": f(inp["ffn1_gate"][0]), "ffn1_up": f(inp["ffn1_up"][0]), "ffn1_down": f(inp["ffn1_down"][0]),
        "ffn2_gate": f(inp["ffn2_gate"][0]), "ffn2_up": f(inp["ffn2_up"][0]), "ffn2_down": f(inp["ffn2_down"][0]),
        "w_in": f(inp["w_in"][0]),
        "pool_w": f(inp["pool_w"][0]),
        "pool_scale": f(inp["pool_scale"][0]).reshape(8, 128),
        "lb_logits": f(inp["lb_logits"]).reshape(16, 128),
        "gnorm_w": f(inp["gnorm_w"][0]).reshape(1, 128),
        "w_out": f(inp["w_out"][0]),
        "ident": ident, "cmask": cmask, "smask": smask, "invcnt": invcnt,
    }
    maps = []
    for i in range(ncores):
        m = dict(shared)
        m["x"] = f(inp["x"][i])[: nblk * T]
        m["sel"] = np.zeros((128, 8), np.float32)
        m["c"] = f(inp["c"][i]).reshape(16, 128)
        maps.append(m)
    return maps


def kernel(**inputs):
    if "nc" not in _CACHE:
        _CACHE["nc"] = build_nc(1, exch=True)[0]
    nc = _CACHE["nc"]
    maps = make_in_maps8(inputs)
    res = run_bass_kernel_spmd(nc, maps, core_ids=list(range(8)))
    out = np.stack([np.asarray(r["out"], dtype=np.float32) for r in res.results], axis=0)
    return np.ascontiguousarray(out.reshape(4, S, D))
```

```python
from contextlib import ExitStack
import numpy as np
import concourse.bass as bass
import concourse.mybir as mybir
from concourse.bass_utils import run_bass_kernel_spmd

F32 = mybir.dt.float32
BF16 = mybir.dt.bfloat16
AF = mybir.ActivationFunctionType
ALU = mybir.AluOpType

D = 2048
S = 2048
T = 1024
KC = 16
DFF = 5632
NFF = 44
EPS = 1e-6
WIN = (2, 4, 8, 16)
FF_GROUPS = ((0, 16), (16, 16), (32, 12))


class Op:
    __slots__ = ("eng", "fn", "deps", "dma", "idx", "sem", "val", "signal", "pre")

    def __init__(self, eng, fn, deps, dma, idx):
        self.eng = eng
        self.fn = fn
        self.deps = deps
        self.dma = dma
        self.idx = idx
        self.sem = None
        self.val = 0
        self.signal = False
        self.pre = None


class _Rec:
    def __init__(self):
        self.call = None

    def __getattr__(self, name):
        def f(*a, **k):
            self.call = (name, a, k)
            return self
        return f


class Prog:
    ENGS = ("pe", "act", "dve", "pool", "sp")
    NDMASEM = 24

    def __init__(self):
        self.ops = []
        self.last_w = {}
        self.readers = {}
        self.touched = {}
        self.fence = {}

    def switch(self, slab):
        t = self.touched.get(slab, set())
        last = {}
        f = set()
        for i in t:
            op = self.ops[i]
            if op.dma:
                f.add(i)
            else:
                if op.eng not in last or last[op.eng] < i:
                    last[op.eng] = i
        f.update(last.values())
        f |= self.fence.get(slab, set()) if not t else set()
        self.fence[slab] = f
        self.touched[slab] = set()
        for d in (self.last_w, self.readers):
            for k in [k for k in d if k[0] == slab]:
                del d[k]

    def add(self, eng, fn, r=(), w=(), dma=False):
        idx = len(self.ops)
        deps = set()
        for k in r:
            lw = self.last_w.get(k)
            if lw is not None:
                deps.add(lw)
            elif k[0] in self.fence:
                deps.update(self.fence[k[0]])
        for k in w:
            lw = self.last_w.get(k)
            if lw is not None:
                deps.add(lw)
            elif k[0] in self.fence:
                deps.update(self.fence[k[0]])
            rd = self.readers.get(k)
            if rd:
                lastc = {}
                for i in rd:
                    o_ = self.ops[i]
                    if o_.dma:
                        deps.add(i)
                    elif lastc.get(o_.eng, -1) < i:
                        lastc[o_.eng] = i
                deps.update(lastc.values())
        for k in r:
            self.readers.setdefault(k, []).append(idx)
            self.touched.setdefault(k[0], set()).add(idx)
        for k in w:
            self.last_w[k] = idx
            self.readers[k] = []
            self.touched.setdefault(k[0], set()).add(idx)
        if fn is not None:
            rec = _Rec()
            fn(rec)
            fn = rec.call
        op = Op(eng, fn, deps, dma, idx)
        self.ops.append(op)
        return op

    def emit(self, nc, stack):
        ops = self.ops
        for op in ops:
            for d in op.deps:
                dop = ops[d]
                if dop.dma:
                    dop.signal = True
                elif dop.eng != op.eng or op.eng != "pe" or op.dma:
                    dop.signal = True
        esem = {e: stack.enter_context(nc.semaphore("s_" + e)) for e in ("pe", "act", "dve", "pool")}
        ndma = {e: sum(1 for op in ops if op.dma and op.dma != "cc" and op.eng == e) for e in ("pool", "sp")}
        nds = {e: max(2, -(-ndma[e] // 13)) for e in ndma}
        dsem = {e: [stack.enter_context(nc.semaphore("d_%s%d" % (e, i))) for i in range(nds[e])]
                for e in ("pool", "sp")}
        cnt = {e: 0 for e in esem}
        dcnt = {e: [0] * nds[e] for e in dsem}
        drr = {e: 0 for e in dsem}
        ccsem = None
        ccn = 0
        for op in ops:
            if op.dma == "cc":
                if ccsem is None:
                    ccsem = stack.enter_context(nc.semaphore("s_cc"))
                ccn += 1
                op.sem = ccsem
                op.val = ccn
                op.signal = True
            elif op.dma:
                e = op.eng
                i = drr[e]
                drr[e] = (i + 1) % nds[e]
                if dcnt[e][i] > 0:
                    op.pre = (dsem[e][i], dcnt[e][i])
                dcnt[e][i] += 16
                op.sem = dsem[e][i]
                op.val = dcnt[e][i]
                op.signal = True
            elif op.signal:
                cnt[op.eng] += 1
                op.sem = esem[op.eng]
                op.val = cnt[op.eng]
        by_eng = {e: [op for op in ops if op.eng == e] for e in self.ENGS}
        nwaits = {e: 0 for e in self.ENGS}

        def run(engname, eng):
            seen = {}
            for op in by_eng[engname]:
                waits = []
                if op.pre is not None:
                    waits.append(op.pre)
                for d in op.deps:
                    dop = ops[d]
                    if not dop.signal:
                        continue
                    if (not dop.dma) and dop.eng == engname and engname == "pe" and not op.dma:
                        continue
                    waits.append((dop.sem, dop.val))
                best = {}
                for s, v in waits:
                    k = id(s)
                    if seen.get(k, 0) >= v:
                        continue
                    if k not in best or best[k][1] < v:
                        best[k] = (s, v)
                for k, (s, v) in best.items():
                    eng.wait_ge(s, v)
                    seen[k] = v
                    nwaits[engname] += 1
                if op.fn is not None:
                    name, a_, k_ = op.fn
                    ins = getattr(eng, name)(*a_, **k_)
                    if op.signal:
                        ins.then_inc(op.sem, 16 if (op.dma and op.dma != "cc") else 1)

        with nc.Block() as block:
            @block.tensor
            def _(e):
                run("pe", e)

            @block.scalar
            def _(e):
                run("act", e)

            @block.vector
            def _(e):
                run("dve", e)

            @block.gpsimd
            def _(e):
                run("pool", e)

            @block.sync
            def _(e):
                run("sp", e)
        self.stats = dict(nops={e: len(by_eng[e]) for e in self.ENGS}, nwaits=nwaits, nsig=cnt)


class Arena:
    def __init__(self, handle, nbytes):
        self.h = handle
        self.n = nbytes
        self.off = 0

    def reset(self):
        self.off = 0

    def alloc(self, free_shape, dt):
        esz = 4 if dt == F32 else 2
        n = int(np.prod(free_shape))
        nb = n * esz
        self.off = (self.off + 31) // 32 * 32
        assert self.off + nb <= self.n, ("arena overflow", self.off, nb, self.n)
        a = self.h[:, self.off // 4:(self.off + nb) // 4]
        self.off += nb
        if dt != F32:
            a = a.bitcast(dt)
        if len(free_shape) == 2:
            a = a.rearrange("p (a b) -> p a b", b=free_shape[1])
        elif len(free_shape) == 3:
            a = a.rearrange("p (a b c) -> p a b c", b=free_shape[1], c=free_shape[2])
        return a


def build_nc(nblk=2, stop=99, dbg=False, exch=False):
    nc = bass.Bass("TRN2", target_bir_lowering=False)
    NTOK = nblk * T

    def din(name, shape):
        return nc.dram_tensor(name, list(shape), F32, kind="ExternalInput").ap()

    x_d = din("x", (NTOK, D))
    c_d = din("c", (16, 128))
    w_ada = din("w_ada", (D, 9 * D)) if not exch else None
    b_ada = din("b_ada", (144, 128))
    nrm_d = din("nrm", (4, 16, 128))
    ffn_d = [(din("ffn1_gate", (D, DFF)), din("ffn1_up", (D, DFF)), din("ffn1_down", (DFF, D))),
             (din("ffn2_gate", (D, DFF)), din("ffn2_up", (D, DFF)), din("ffn2_down", (DFF, D)))]
    w_in = din("w_in", (D, 5120))
    pool_w = din("pool_w", (4, 256, 256))
    pscale_d = din("pool_scale", (8, 128))
    lb_d = din("lb_logits", (16, 128))
    gn_d = din("gnorm_w", (1, 128))
    w_out = din("w_out", (D, D))
    ident_d = din("ident", (128, 128))
    cmask_d = din("cmask", (128, 128))
    smask_d = din("smask", (128, 512))
    invcnt_d = din("invcnt", (128, 64))
    sel_d = din("sel", (128, 8))
    if exch:
        call_d = din("c_all", (64, 128))
        wsl_d = din("w_ada_sl", (D, 2304))
        selb_d = din("selb", (128, 4))
        cin2_d = nc.dram_tensor("cin2", [72, 128], F32).ap()
        cout2_d = nc.dram_tensor("cout2", [576, 128], F32).ap()
    cin_d = nc.dram_tensor("cin", [1152, 128], F32).ap()
    cout_d = nc.dram_tensor("cout", [8 * 1152, 128], F32).ap()
    out_d = nc.dram_tensor("out", [NTOK, D], F32, kind="ExternalOutput").ap()
    dbg_d = nc.dram_tensor("dbg", [128, KC, T], F32, kind="ExternalOutput").ap() if dbg else None

    p = Prog()
    st = ExitStack()
    with st:
        def sb(name, shape, dt):
            return st.enter_context(nc.sbuf_tensor("sb_" + name, list(shape), dt))

        xT = sb("xT", [128, KC, T], F32)
        hT = sb("hT", [128, KC, T], BF16)
        slabA_h = sb("slabA", [128, 8192], F32)
        slabA = Arena(slabA_h, 32768)
        NW = 3
        wpool = [sb("wp%d" % i, [128, 4096], BF16) for i in range(NW)]
        NS32, NS16 = 6, 5
        s32 = [sb("s32_%d" % i, [128, 528], F32) for i in range(NS32)]
        s16 = [sb("s16_%d" % i, [128, 512], BF16) for i in range(NS16)]
        ps_h = st.enter_context(nc.psum_tensor("ps", [128, 8, 512], F32))
        ident_f = sb("ident_f", [128, 128], F32)
        ident_b = sb("ident_b", [128, 128], BF16)
        ones_b = sb("ones_b", [128, 128], BF16)
        cmask = sb("cmask", [128, 128], F32)
        smask = sb("smask", [128, 512], F32)
        invcnt = sb("invcnt", [128, 64], F32)
        sel = sb("sel", [128, 8], F32)
        stg = sb("stg", [128, 128], F32)
        colsT = sb("colsT", [128, 128], F32)
        badaT = sb("badaT", [128, 144], F32)
        modT = sb("modT", [128, 144], F32)
        vecs = sb("vecs", [128, 9, 16], F32)
        cT = sb("cT", [128, 16], BF16)
        cT4 = sb("cT4", [128, 16, 4], BF16)
        selb = sb("selb", [128, 4], F32)
        lbv = sb("lbv", [128, 3, 8], F32)
        epsc = sb("epsc", [128, 1], F32)
        poolw_sb = sb("poolw", [128, 8, 256], BF16)
        S_carry = sb("S_carry", [128, 8, 128], F32)
        halo = sb("halo", [128, 8, 16], F32)
        eb = sb("eb", [128, 1024], F32)
        kt = sb("kt", [128, 1024], BF16)
        kh = sb("kh", [128, 1024], BF16)
        Apair = sb("Apair", [128, 8, 128], BF16)
        kA = sb("kA", [128, 8, 128], BF16)
        kB = sb("kB", [128, 8, 128], BF16)
        Spp = sb("Spp", [128, 2, 128], F32)
        Sbf = sb("Sbf", [128, 16, 128], BF16)
        rstd2 = [sb("rstd%d" % i, [128, 512], F32) for i in range(2)]

        rr = {"ps": 0, "w": 0, "s32": 0, "s16": 0}

        def psb():
            i = rr["ps"]
            rr["ps"] = (i + 1) % 8
            return i, ps_h[:, i, :], ("ps", i)

        def wtile():
            i = rr["w"]
            rr["w"] = (i + 1) % NW
            return wpool[i], ("w", i)

        def sc32():
            i = rr["s32"]
            rr["s32"] = (i + 1) % NS32
            return s32[i], ("s32", i)

        def sc16():
            i = rr["s16"]
            rr["s16"] = (i + 1) % NS16
            return s16[i], ("s16", i)

        flip = {"v": 0}

        def evac_eng():
            flip["v"] ^= 1
            return "act" if flip["v"] else "dve"

        def copy_on(eng, out, in_):
            if eng == "act":
                return lambda e: e.copy(out=out, in_=in_)
            return lambda e: e.tensor_copy(out=out, in_=in_)

        for (t_, d_, k_) in ((ident_f, ident_d, "ident_f"), (cmask, cmask_d, "cmask"), (smask, smask_d, "smask"),
                             (invcnt, invcnt_d, "invcnt"), (sel, sel_d, "sel")):
            p.add("sp", lambda e, t_=t_, d_=d_: e.dma_start(out=t_[:], in_=d_[:, :]), w=[("c", k_)], dma=True)
        p.add("dve", lambda e: e.tensor_copy(out=ident_b[:], in_=ident_f[:]), r=[("c", "ident_f")], w=[("c", "ident_b")])
        p.add("dve", lambda e: e.memset(ones_b[:], 1.0), w=[("c", "ones")])
        p.add("dve", lambda e: e.memset(epsc[:], EPS), w=[("c", "eps")])
        p.add("dve", lambda e: e.memset(stg[:], 0.0), w=[("c", "stg")])
        p.add("dve", lambda e: e.memset(S_carry[:], 0.0), w=[("c", "Sc", h) for h in range(8)])
        p.add("dve", lambda e: e.memset(halo[:], 0.0), w=[("c", "halo", c) for c in range(8)])
        p.add("dve", lambda e: e.memset(kA[:], 0.0), w=[("m", "kA")])
        p.add("dve", lambda e: e.memset(kB[:], 0.0), w=[("m", "kB")])
        p.add("pool", lambda e: e.dma_start(out=poolw_sb[:], in_=pool_w.rearrange("g (kc p) n -> p (g kc) n", p=128)),
              w=[("c", "poolw")], dma=True)
        rows = [(nrm_d.rearrange("a b n -> (a b) n"), 0, 64), (pscale_d, 64, 8), (lb_d, 72, 16), (gn_d, 88, 1),
                (c_d, 89, 16)]
        for (src, r0, n) in rows:
            p.add("sp", lambda e, src=src, r0=r0, n=n: e.dma_start(out=stg[r0:r0 + n, :], in_=src[:, :]),
                  r=[("c", "stg")], w=[("c", "stgr", r0)], dma=True)
        bi, bank, bk = psb()
        p.add("pe", lambda e, bank=bank: e.transpose(out=bank[:, 0:128], in_=stg[:], identity=ident_f[:]),
              r=[("c", "stgr", r0) for (_, r0, _) in rows] + [("c", "ident_f")], w=[bk])
        p.add("dve", lambda e, bank=bank: e.tensor_copy(out=colsT[:], in_=bank[:, 0:128]), r=[bk], w=[("c", "colsT")])
        for (r0, n) in ((0, 128), (128, 16)):
            s_, sk = sc32()
            p.add("sp", lambda e, s_=s_, r0=r0, n=n: e.dma_start(out=s_[0:n, 0:128], in_=b_ada[r0:r0 + n, :]),
                  w=[sk], dma=True)
            bi, bank, bk = psb()
            p.add("pe", lambda e, bank=bank, s_=s_, n=n: e.transpose(out=bank[:, 0:n], in_=s_[0:n, 0:128],
                                                                      identity=ident_f[0:n, 0:n]),
                  r=[sk, ("c", "ident_f")], w=[bk])
            p.add("dve", lambda e, bank=bank, r0=r0, n=n: e.tensor_copy(out=badaT[:, r0:r0 + n], in_=bank[:, 0:n]),
                  r=[bk], w=[("c", "badaT", r0)])
        p.add("act", lambda e: e.activation(out=cT[:], in_=colsT[:, 89:105], func=AF.Silu), r=[("c", "colsT")], w=[("c", "cT")])
        p.add("dve", lambda e: e.tensor_tensor(out=lbv[:, 0, :], in0=colsT[:, 72:80], in1=colsT[:, 80:88], op=ALU.subtract),
              r=[("c", "colsT")], w=[("c", "lbd")])
        p.add("act", lambda e: e.activation(out=lbv[:, 1, :], in_=lbv[:, 0, :], func=AF.Sigmoid), r=[("c", "lbd")], w=[("c", "oml")])
        p.add("dve", lambda e: e.tensor_scalar(out=lbv[:, 2, :], in0=lbv[:, 1, :], scalar1=-1.0, scalar2=None, op0=ALU.mult),
              r=[("c", "oml")], w=[("c", "noml")])
        nw = lambda i: colsT[:, 16 * i:16 * (i + 1)]
        pscale = colsT[:, 64:72]
        gnw = colsT[:, 88:89]

        wada_v = w_ada.rearrange("(kc p) n -> p kc n", p=128) if not exch else None

        def A_(g):
            return vecs[:, 3 * g + 1, :], ("c", "vec", 3 * g + 1)

        def SH_(g):
            return vecs[:, 3 * g, :], ("c", "vec", 3 * g)

        def G_(g):
            return vecs[:, 3 * g + 2, :], ("c", "vec", 3 * g + 2)

        def emit_mod2(seg):
            bi, bank, bk = psb()
            for tt in range(8):
                wt, wk = wtile()
                col0 = seg * D + tt * 256
                wv = wt[:].rearrange("p (kc n) -> p kc n", n=256)
                p.add("pool", lambda e, wv=wv, col0=col0: e.dma_start(out=wv, in_=wada_v[:, :, col0:col0 + 256]), w=[wk], dma=True)
                for jj in range(2):
                    j = tt * 2 + jj
                    for kc in range(KC):
                        p.add("pe", lambda e, bank=bank, wv=wv, jj=jj, kc=kc, j=j: e.matmul(
                            bank[:, j:j + 1], lhsT=wv[:, kc, jj * 128:(jj + 1) * 128], rhs=cT[:, kc:kc + 1],
                            start=(kc == 0), stop=(kc == KC - 1)), r=[wk, ("c", "cT")], w=[bk])
            msl = modT[:, seg * 16:(seg + 1) * 16]
            p.add("dve", lambda e, bank=bank: e.tensor_tensor(out=msl, in0=bank[:, 0:16], in1=badaT[:, seg * 16:(seg + 1) * 16],
                                                              op=ALU.add),
                  r=[bk, ("c", "badaT", 0), ("c", "badaT", 128)], w=[("c", "mod", seg)])
            g = seg // 3
            kind = seg % 3
            dst = vecs[:, seg, :]
            if kind == 0:
                p.add("dve", lambda e: e.tensor_copy(out=dst, in_=msl), r=[("c", "mod", seg)], w=[("c", "vec", seg)])
            elif kind == 1:
                p.add("dve", lambda e: e.scalar_tensor_tensor(out=dst, in0=msl, scalar=1.0, in1=nw(g), op0=ALU.add, op1=ALU.mult),
                      r=[("c", "mod", seg), ("c", "colsT")], w=[("c", "vec", seg)])
            else:
                fac = 1.0 if g == 1 else 0.5
                p.add("dve", lambda e: e.tensor_scalar(out=dst, in0=msl, scalar1=fac, scalar2=None, op0=ALU.mult),
                      r=[("c", "mod", seg)], w=[("c", "vec", seg)])

        def emit_mod_all():
            modrow = slabA.alloc([2304], F32)
            p.add("sp", lambda e: e.dma_start(out=selb[:], in_=selb_d[:, :]), w=[("c", "selb")], dma=True)
            cs_, csk = sc32()
            p.add("sp", lambda e: e.dma_start(out=cs_[0:64, 0:128], in_=call_d[:, :]), w=[csk], dma=True)
            bi, bank, bk = psb()
            p.add("pe", lambda e: e.transpose(out=bank[:, 0:64], in_=cs_[0:64, 0:128], identity=ident_f[0:64, 0:64]),
                  r=[csk, ("c", "ident_f")], w=[bk])
            p.add("act", lambda e: e.activation(out=cT4[:].rearrange("p k b -> p b k"),
                                                in_=bank[:, 0:64].rearrange("p (b k) -> p b k", k=16), func=AF.Silu),
                  r=[bk], w=[("c", "cT4")])
            wsl_v = wsl_d.rearrange("(kc p) n -> p kc n", p=128)
            for tt in range(9):
                wt, wk = wtile()
                wv = wt[:].rearrange("p (kc n) -> p kc n", n=256)
                p.add("pool", lambda e, wv=wv, tt=tt: e.dma_start(out=wv, in_=wsl_v[:, :, tt * 256:(tt + 1) * 256]), w=[wk], dma=True)
                if tt % 2 == 0:
                    bi, bank, bk = psb()
                for kc in range(KC):
                    p.add("pe", lambda e, bank=bank, wv=wv, kc=kc, tt=tt: e.matmul(
                        bank[0:4, (tt % 2) * 256:(tt % 2 + 1) * 256], lhsT=cT4[:, kc, :], rhs=wv[:, kc, :],
                        start=(kc == 0), stop=(kc == KC - 1)), r=[wk, ("c", "cT4")], w=[bk])
                if tt % 2 == 1 or tt == 8:
                    c0 = (tt // 2) * 512
                    n_ = 512 if tt % 2 == 1 else 256
                    p.add("dve", lambda e, bank=bank, c0=c0, n_=n_: e.tensor_copy(out=modrow[0:4, c0:c0 + n_], in_=bank[0:4, 0:n_]),
                          r=[bk], w=[("A", "modrow", tt // 2)])
            p.add("pool", lambda e: e.dma_start(out=cin2_d.rearrange("(jj b) q -> b jj q", b=4),
                                                in_=modrow[0:4, :].rearrange("b (jj q) -> b jj q", q=128)),
                  r=[("A", "modrow", i) for i in range(5)], w=[("dram", "cin2")], dma=True)
            p.add("pool", lambda e: e.collective_compute("AllGather", ALU.bypass, replica_groups=[list(range(8))],
                                                         ins=[cin2_d.opt()], outs=[cout2_d.opt()]),
                  r=[("dram", "cin2")], w=[("dram", "cout2")], dma="cc")
            p.add("dve", lambda e: e.tensor_copy(out=modT[:], in_=badaT[:]), r=[("c", "badaT", 0), ("c", "badaT", 128)],
                  w=[("c", "modT")])
            gview = cout2_d.rearrange("(j b) q -> b j q", b=4)
            for b in range(4):
                for (r0, n) in ((0, 128), (128, 16)):
                    s_, sk = sc32()
                    p.add("pool", lambda e, s_=s_, b=b, r0=r0, n=n: e.dma_start(out=s_[0:n, 0:128], in_=gview[b, r0:r0 + n, :]),
                          r=[("dram", "cout2")], w=[sk], dma=True)
                    bi, bank, bk = psb()
                    p.add("pe", lambda e, bank=bank, s_=s_, n=n: e.transpose(out=bank[:, 0:n], in_=s_[0:n, 0:128],
                                                                              identity=ident_f[0:n, 0:n]),
                          r=[sk, ("c", "ident_f")], w=[bk])
                    p.add("dve", lambda e, bank=bank, b=b, r0=r0, n=n: e.scalar_tensor_tensor(
                        out=modT[:, r0:r0 + n], in0=bank[:, 0:n], scalar=selb[:, b:b + 1], in1=modT[:, r0:r0 + n],
                        op0=ALU.mult, op1=ALU.add), r=[bk, ("c", "selb"), ("c", "modT")], w=[("c", "modT")])
            for seg in range(9):
                g = seg // 3
                kind = seg % 3
                msl = modT[:, seg * 16:(seg + 1) * 16]
                dst = vecs[:, seg, :]
                if kind == 0:
                    p.add("dve", lambda e, dst=dst, msl=msl: e.tensor_copy(out=dst, in_=msl), r=[("c", "modT")], w=[("c", "vec", seg)])
                elif kind == 1:
                    p.add("dve", lambda e, dst=dst, msl=msl, g=g: e.scalar_tensor_tensor(out=dst, in0=msl, scalar=1.0, in1=nw(g),
                                                                                          op0=ALU.add, op1=ALU.mult),
                          r=[("c", "modT"), ("c", "colsT")], w=[("c", "vec", seg)])
                else:
                    fac = 1.0 if g == 1 else 0.5
                    p.add("dve", lambda e, dst=dst, msl=msl, fac=fac: e.tensor_scalar(out=dst, in0=msl, scalar1=fac, scalar2=None,
                                                                                       op0=ALU.mult),
                          r=[("c", "modT")], w=[("c", "vec", seg)])

        def xk(kc, half):
            return ("x", kc, half)

        def hk(kc, half):
            return ("h", kc, half)

        def hs(half):
            return slice(half * 512, (half + 1) * 512)

        def emit_rstd(bank, bk, scale, out_, ok):
            l_, lk = sc32()
            p.add("act", lambda e: e.activation(out=l_[:, 0:512], in_=bank, func=AF.Ln, bias=epsc[:, 0:1], scale=scale),
                  r=[bk, ("c", "eps")], w=[lk])
            p.add("act", lambda e: e.activation(out=out_, in_=l_[:, 0:512], func=AF.Exp, scale=-0.5), r=[lk], w=[ok])

        def emit_norm_mod(g, part=3):
            a_, ak = A_(g)
            sh_, shk = SH_(g)
            for half in range(2):
                if not (part & 1):
                    continue
                bi, bank, bk = psb()
                for kc in range(KC):
                    q_, qk = sc16()
                    p.add("act", lambda e, q_=q_, kc=kc: e.activation(out=q_[:], in_=xT[:, kc, hs(half)], func=AF.Square),
                          r=[xk(kc, half)], w=[qk])
                    p.add("pe", lambda e, q_=q_, kc=kc, bank=bank: e.matmul(bank, lhsT=ones_b[:], rhs=q_[:], start=(kc == 0),
                                                                            stop=(kc == KC - 1)),
                          r=[qk, ("c", "ones")], w=[bk])
                rs_, rk = rstd2[half], ("c", "rstd", half)
                emit_rstd(bank, bk, 1.0 / D, rs_[:, 0:512], rk)
            for half in range(2):
                if not (part & 2):
                    continue
                rs_, rk = rstd2[half], ("c", "rstd", half)
                for kc in range(KC):
                    t_, tk = sc32()
                    p.add("dve", lambda e, t_=t_, kc=kc: e.scalar_tensor_tensor(
                        out=t_[:, 0:512], in0=xT[:, kc, hs(half)], scalar=a_[:, kc:kc + 1], in1=rs_[:, 0:512],
                        op0=ALU.mult, op1=ALU.mult), r=[xk(kc, half), rk, ak], w=[tk])
                    p.add("act", lambda e, t_=t_, kc=kc: e.activation(out=hT[:, kc, hs(half)], in_=t_[:, 0:512], func=AF.Identity,
                                                                      bias=sh_[:, kc:kc + 1]),
                          r=[tk, shk], w=[hk(kc, half)])

        def emit_ffn(fi, g):
            wg_d, wu_d, wd_d = ffn_d[fi]
            wg_v = wg_d.rearrange("(kc p) n -> p kc n", p=128)
            wu_v = wu_d.rearrange("(kc p) n -> p kc n", p=128)
            wd_v = wd_d.rearrange("(c p) n -> p c n", p=128)
            hg_, hgk = G_(g)
            act = slabA_h[:, :].bitcast(BF16).rearrange("p (c t) -> p c t", t=T)
            for (c0, nch) in FF_GROUPS:
                for cp in range(nch // 2):
                    wgt, wgk = wtile()
                    wut, wuk = wtile()
                    col0 = (c0 + 2 * cp) * 128
                    wgv = wgt[:].rearrange("p (kc n) -> p kc n", n=256)
                    wuv = wut[:].rearrange("p (kc n) -> p kc n", n=256)
                    p.add("pool", lambda e, wgv=wgv, col0=col0: e.dma_start(out=wgv, in_=wg_v[:, :, col0:col0 + 256]), w=[wgk], dma=True)
                    p.add("pool", lambda e, wuv=wuv, col0=col0: e.dma_start(out=wuv, in_=wu_v[:, :, col0:col0 + 256]), w=[wuk], dma=True)
                    sil = {}
                    for cc in range(2):
                        for half in range(2):
                            gi_, gbank, gk = psb()
                            for kc in range(KC):
                                p.add("pe", lambda e, gbank=gbank, wgv=wgv, cc=cc, kc=kc, half=half: e.matmul(
                                    gbank, lhsT=wgv[:, kc, cc * 128:(cc + 1) * 128], rhs=hT[:, kc, hs(half)],
                                    start=(kc == 0), stop=(kc == KC - 1)), r=[wgk, hk(kc, half)], w=[gk])
                            s_, sk = sc16()
                            p.add("act", lambda e, s_=s_, gbank=gbank: e.activation(out=s_[:], in_=gbank, func=AF.Silu), r=[gk], w=[sk])
                            sil[(cc, half)] = (s_, sk)
                    for cc in range(2):
                        cl = 2 * cp + cc
                        for half in range(2):
                            ui_, ubank, uk = psb()
                            for kc in range(KC):
                                p.add("pe", lambda e, ubank=ubank, wuv=wuv, cc=cc, kc=kc, half=half: e.matmul(
                                    ubank, lhsT=wuv[:, kc, cc * 128:(cc + 1) * 128], rhs=hT[:, kc, hs(half)],
                                    start=(kc == 0), stop=(kc == KC - 1)), r=[wuk, hk(kc, half)], w=[uk])
                            s_, sk = sil[(cc, half)]
                            p.add("dve", lambda e, s_=s_, ubank=ubank, cl=cl, half=half: e.tensor_tensor(
                                out=act[:, cl, hs(half)], in0=s_[:], in1=ubank, op=ALU.mult), r=[sk, uk], w=[("A", cl, half)])
                for dp in range(8):
                    wdt, wdk = wtile()
                    wdv = wdt[:, 0:nch * 256].rearrange("p (c n) -> p c n", n=256)
                    p.add("pool", lambda e, wdv=wdv, dp=dp, c0=c0, nch=nch: e.dma_start(
                        out=wdv, in_=wd_v[:, c0:c0 + nch, dp * 256:(dp + 1) * 256]), w=[wdk], dma=True)
                    for dc in range(2):
                        dch = dp * 2 + dc
                        for half in range(2):
                            bi_, bank, bk = psb()
                            for j in range(nch):
                                p.add("pe", lambda e, bank=bank, wdv=wdv, dc=dc, j=j, half=half: e.matmul(
                                    bank, lhsT=wdv[:, j, dc * 128:(dc + 1) * 128], rhs=act[:, j, hs(half)],
                                    start=(j == 0), stop=(j == nch - 1)), r=[wdk, ("A", j, half)], w=[bk])
                            p.add("dve", lambda e, bank=bank, dch=dch, half=half: e.scalar_tensor_tensor(
                                out=xT[:, dch, hs(half)], in0=bank, scalar=hg_[:, dch:dch + 1], in1=xT[:, dch, hs(half)],
                                op0=ALU.mult, op1=ALU.add), r=[bk, hgk, xk(dch, half)], w=[xk(dch, half)])

        def emit_load_x(blk):
            slabA.reset()
            stgx = [slabA.alloc([2048], F32) for _ in range(2)]
            for j in range(8):
                sx = stgx[j % 2]
                skey = ("A", "stgx", j % 2)
                r0 = blk * T + j * 128
                p.add("sp", lambda e, sx=sx, r0=r0: e.dma_start(out=sx, in_=x_d[r0:r0 + 128, :]), w=[skey], dma=True)
                for q4 in range(4):
                    bi_, bank, bk = psb()
                    for i in range(4):
                        kc = q4 * 4 + i
                        p.add("pe", lambda e, bank=bank, sx=sx, kc=kc, i=i: e.transpose(
                            out=bank[:, i * 128:(i + 1) * 128], in_=sx[:, kc * 128:(kc + 1) * 128], identity=ident_f[:]),
                            r=[skey, ("c", "ident_f")], w=[bk])
                    eng = evac_eng()
                    p.add(eng, copy_on(eng, xT[:, q4 * 4:q4 * 4 + 4, j * 128:(j + 1) * 128],
                                       bank.rearrange("p (a b) -> p a b", b=128)),
                          r=[bk], w=[xk(q4 * 4 + i, j // 4) for i in range(4)])

        def emit_final(blk):
            slabA.reset()
            stgo = [slabA.alloc([2048], F32) for _ in range(2)]
            fw = nw(3)
            rs = []
            for half in range(2):
                bi, bank, bk = psb()
                for kc in range(KC):
                    q_, qk = sc16()
                    p.add("act", lambda e, q_=q_, kc=kc, half=half: e.activation(out=q_[:], in_=xT[:, kc, hs(half)], func=AF.Square),
                          r=[xk(kc, half)], w=[qk])
                    p.add("pe", lambda e, q_=q_, kc=kc, bank=bank: e.matmul(bank, lhsT=ones_b[:], rhs=q_[:], start=(kc == 0),
                                                                            stop=(kc == KC - 1)), r=[qk, ("c", "ones")], w=[bk])
                rs_, rk = sc32()
                emit_rstd(bank, bk, 1.0 / D, rs_[:, 0:512], rk)
                for kc in range(KC):
                    p.add("dve", lambda e, kc=kc, half=half, rs_=rs_: e.scalar_tensor_tensor(
                        out=xT[:, kc, hs(half)], in0=xT[:, kc, hs(half)], scalar=fw[:, kc:kc + 1], in1=rs_[:, 0:512],
                        op0=ALU.mult, op1=ALU.mult), r=[xk(kc, half), rk, ("c", "colsT")], w=[xk(kc, half)])
            for j in range(8):
                so = stgo[j % 2]
                okey = ("A", "stgo", j % 2)
                for q4 in range(4):
                    bi_, bank, bk = psb()
                    for i in range(4):
                        kc = q4 * 4 + i
                        p.add("pe", lambda e, bank=bank, kc=kc, i=i, j=j: e.transpose(
                            out=bank[:, i * 128:(i + 1) * 128], in_=xT[:, kc, j * 128:(j + 1) * 128], identity=ident_f[:]),
                            r=[xk(kc, j // 4), ("c", "ident_f")], w=[bk])
                    eng = evac_eng()
                    p.add(eng, copy_on(eng, so[:, q4 * 512:(q4 + 1) * 512], bank), r=[bk], w=[okey])
                r0 = blk * T + j * 128
                p.add("sp", lambda e, so=so, r0=r0: e.dma_start(out=out_d[r0:r0 + 128, :], in_=so), r=[okey], w=[("o", blk, j)], dma=True)

        win_v = w_in.rearrange("(kc p) n -> p kc n", p=128)
        wout_v = w_out.rearrange("(kc p) n -> p kc n", p=128)

        def emit_wout(ymix, rowhalf, ykeyf):
            g2_, g2k = G_(1)
            for dp in range(8):
                wt, wk = wtile()
                wv = wt[:, 0:2048].rearrange("p (c n) -> p c n", n=256)
                p.add("pool", lambda e, wv=wv, dp=dp: e.dma_start(
                    out=wv, in_=wout_v[:, rowhalf * 8:rowhalf * 8 + 8, dp * 256:(dp + 1) * 256]), w=[wk], dma=True)
                for dc in range(2):
                    dch = dp * 2 + dc
                    for half in range(2):
                        bi_, bank, bk = psb()
                        for j in range(8):
                            p.add("pe", lambda e, bank=bank, wv=wv, dc=dc, j=j, half=half: e.matmul(
                                bank, lhsT=wv[:, j, dc * 128:(dc + 1) * 128], rhs=ymix[:, j, hs(half)],
                                start=(j == 0), stop=(j == 7)), r=[wk, ykeyf(j, half)], w=[bk])
                        p.add("dve", lambda e, bank=bank, dch=dch, half=half: e.scalar_tensor_tensor(
                            out=xT[:, dch, hs(half)], in0=bank, scalar=g2_[:, dch:dch + 1], in1=xT[:, dch, hs(half)],
                            op0=ALU.mult, op1=ALU.add), r=[bk, g2k, xk(dch, half)], w=[xk(dch, half)])

        def emit_mixer(blk, state=False):
            slabA.reset()
            ymix = slabA.alloc([8, T], BF16)
            qt2 = [slabA.alloc([T], BF16) for _ in range(2)]
            vt2 = [slabA.alloc([8, 128], BF16) for _ in range(2)]
            sgz2 = [slabA.alloc([T], BF16) for _ in range(2)]
            pooled = slabA.alloc([2, T], BF16)
            if state:
                gsc = ymix[:, 0:3, :].bitcast(F32).rearrange("p a (b c) -> p (a b) c", c=128)[:, 0:9, :]
                halo_new = pooled[:, 0, 0:256].bitcast(F32).rearrange("p (c t) -> p c t", t=16)

            def yk(c, half):
                return ("A", "y", c, half)

            if state:
                for gi in range(4):
                    wt, wk = wtile()
                    wv = wt[:].rearrange("p (kc n) -> p kc n", n=256)
                    p.add("pool", lambda e, wv=wv, gi=gi: e.dma_start(out=wv, in_=win_v[:, :, gi * 256:(gi + 1) * 256]), w=[wk], dma=True)
                    bi_, bank, bk = psb()
                    for cc in range(2):
                        for kc in range(KC):
                            p.add("pe", lambda e, bank=bank, wv=wv, cc=cc, kc=kc: e.matmul(
                                bank[:, cc * 16:(cc + 1) * 16], lhsT=wv[:, kc, cc * 128:(cc + 1) * 128], rhs=hT[:, kc, T - 16:T],
                                start=(kc == 0), stop=(kc == KC - 1)), r=[wk, hk(kc, 1)], w=[bk])
                    p.add("dve", lambda e, bank=bank, gi=gi: e.tensor_copy(
                        out=halo_new[:, 2 * gi:2 * gi + 2, :], in_=bank[:, 0:32].rearrange("p (a b) -> p a b", b=16)),
                        r=[bk], w=[("A", "halo_new", gi)])

            def emit_pool_branch():
              for gi in range(4):
                  w = WIN[gi]
                  nlev = gi + 1
                  wt, wk = wtile()
                  wv = wt[:].rearrange("p (kc n) -> p kc n", n=256)
                  p.add("pool", lambda e, wv=wv, gi=gi: e.dma_start(out=wv, in_=win_v[:, :, gi * 256:(gi + 1) * 256]), w=[wk], dma=True)
                  for cc in range(2):
                      c = 2 * gi + cc
                      for half in range(2):
                          bi_, bank, bk = psb()
                          for kc in range(KC):
                              p.add("pe", lambda e, bank=bank, wv=wv, cc=cc, kc=kc, half=half: e.matmul(
                                  bank, lhsT=wv[:, kc, cc * 128:(cc + 1) * 128], rhs=hT[:, kc, hs(half)],
                                  start=(kc == 0), stop=(kc == KC - 1)), r=[wk, hk(kc, half)], w=[bk])
                          u_, uk = sc32()
                          p.add("act", lambda e, u_=u_, bank=bank: e.copy(out=u_[:, 16:528], in_=bank), r=[bk], w=[uk])
                          p.add("dve", lambda e, u_=u_, c=c: e.tensor_copy(out=u_[:, 0:16], in_=halo[:, c, :]),
                                r=[("c", "halo", c), uk], w=[uk])
                          p.add("dve", lambda e, u_=u_, c=c: e.tensor_copy(out=halo[:, c, :], in_=u_[:, 512:528]),
                                r=[uk], w=[("c", "halo", c)])
                          cur, ck = u_, uk
                          sh = 1
                          for lv in range(nlev):
                              n_, nk = sc32()
                              lo = 2 * sh - 1
                              p.add("dve", lambda e, n_=n_, cur=cur, lo=lo, sh=sh: e.tensor_tensor(
                                  out=n_[:, lo:528], in0=cur[:, lo:528], in1=cur[:, lo - sh:528 - sh], op=ALU.add),
                                  r=[ck], w=[nk])
                              cur, ck = n_, nk
                              sh *= 2
                          p.add("dve", lambda e, cur=cur, u_=u_, cc=cc, half=half, w=w: e.scalar_tensor_tensor(
                              out=pooled[:, cc, hs(half)], in0=cur[:, 16:528], scalar=1.0 / w, in1=u_[:, 16:528],
                              op0=ALU.mult, op1=ALU.subtract), r=[ck, uk], w=[("A", "pl", cc, half)])
                          if blk == 0 and half == 0:
                              t_, tk = sc32()
                              p.add("dve", lambda e, t_=t_, cur=cur, gi=gi: e.tensor_tensor(
                                  out=t_[:, 0:16], in0=cur[:, 16:32], in1=invcnt[:, gi * 16:(gi + 1) * 16], op=ALU.mult),
                                  r=[ck, ("c", "invcnt")], w=[tk])
                              p.add("dve", lambda e, t_=t_, u_=u_, cc=cc: e.tensor_tensor(
                                  out=pooled[:, cc, 0:16], in0=t_[:, 0:16], in1=u_[:, 16:32], op=ALU.subtract),
                                  r=[tk, uk], w=[("A", "pl", cc, 0)])
                  for dc in range(2):
                      co = 2 * gi + dc
                      for half in range(2):
                          bi_, bank, bk = psb()
                          for kc in range(2):
                              p.add("pe", lambda e, bank=bank, gi=gi, kc=kc, dc=dc, half=half: e.matmul(
                                  bank, lhsT=poolw_sb[:, gi * 2 + kc, dc * 128:(dc + 1) * 128], rhs=pooled[:, kc, hs(half)],
                                  start=(kc == 0), stop=(kc == 1)), r=[("c", "poolw"), ("A", "pl", kc, half)], w=[bk])
                          p.add("act", lambda e, bank=bank, co=co, half=half: e.activation(
                              out=ymix[:, co, hs(half)], in_=bank, func=AF.Identity, scale=pscale[:, co:co + 1]),
                              r=[bk, ("c", "colsT")], w=[yk(co, half)])

            if stop == 31:
                emit_pool_branch()
                emit_wout(ymix, 0, yk)
                return

            def wtile128(col0):
                wt, wk = wtile()
                wv = wt[:, 0:2048].rearrange("p (kc n) -> p kc n", n=128)
                p.add("pool", lambda e: e.dma_start(out=wv, in_=win_v[:, :, col0:col0 + 128]), w=[wk], dma=True)
                return wv, wk

            def proj_fm(wv, wk, half):
                bi_, bank, bk = psb()
                for kc in range(KC):
                    p.add("pe", lambda e, kc=kc: e.matmul(bank, lhsT=wv[:, kc, :], rhs=hT[:, kc, hs(half)],
                                                          start=(kc == 0), stop=(kc == KC - 1)), r=[wk, hk(kc, half)], w=[bk])
                return bank, bk

            def emit_proj(h):
                db = h % 2
                qt, vt, sgz = qt2[db], vt2[db], sgz2[db]
                oml = lbv[:, 1, h:h + 1]
                noml = lbv[:, 2, h:h + 1]
                wv, wk = wtile128(2048 + h * 128)
                for half in range(2):
                    bank, bk = proj_fm(wv, wk, half)
                    sg_, sgk = sc32()
                    p.add("act", lambda e, sg_=sg_, bank=bank: e.activation(out=sg_[:, 0:512], in_=bank, func=AF.Sigmoid, scale=-1.0),
                          r=[bk], w=[sgk])
                    gb_, gbk = sc32()
                    p.add("act", lambda e, sg_=sg_, gb_=gb_: e.activation(out=gb_[:, 0:512], in_=sg_[:, 0:512], func=AF.Ln,
                                                                          bias=1.0, scale=noml), r=[sgk, ("c", "noml")], w=[gbk])
                    b_, bk2 = sc32()
                    p.add("dve", lambda e, gb_=gb_, b_=b_: e.tensor_tensor_scan(
                        out=b_[:, 0:512], data0=smask[:], data1=gb_[:, 0:512], initial=0.0, op0=ALU.mult, op1=ALU.add),
                        r=[gbk, ("c", "smask")], w=[bk2])
                    en_, enk = sc32()
                    p.add("act", lambda e, b_=b_, en_=en_: e.activation(out=en_[:, 0:512], in_=b_[:, 0:512], func=AF.Exp, scale=-1.0),
                          r=[bk2], w=[enk])
                    p.add("dve", lambda e, sg_=sg_, en_=en_, half=half: e.scalar_tensor_tensor(
                        out=kt[:, hs(half)], in0=sg_[:, 0:512], scalar=oml, in1=en_[:, 0:512], op0=ALU.mult, op1=ALU.mult),
                        r=[sgk, enk, ("c", "oml")], w=[("m", "kt", half)])
                    p.add("act", lambda e, b_=b_, half=half: e.activation(out=eb[:, hs(half)], in_=b_[:, 0:512], func=AF.Exp),
                          r=[bk2], w=[("m", "eb", half)])
                    ebv = eb[:, hs(half)].rearrange("p (a b) -> p a b", b=64)
                    p.add("dve", lambda e, half=half, ebv=ebv: e.tensor_tensor(
                        out=kh[:, hs(half)].rearrange("p (a b) -> p a b", b=64),
                        in0=kt[:, hs(half)].rearrange("p (a b) -> p a b", b=64),
                        in1=ebv[:, :, 63:64].to_broadcast([128, 8, 64]), op=ALU.mult),
                        r=[("m", "kt", half), ("m", "eb", half)], w=[("m", "kh", half)])
                wv, wk = wtile128(3072 + h * 128)
                vis = []
                for half in range(2):
                    bank, bk = proj_fm(wv, wk, half)
                    vi_, vik = sc16()
                    eng = evac_eng()
                    p.add(eng, copy_on(eng, vi_[:], bank), r=[bk], w=[vik])
                    vis.append((vi_, vik))
                wv, wk = wtile128(1024 + h * 128) if not state else (None, None)
                for half in (range(2) if not state else ()):
                    bank, bk = proj_fm(wv, wk, half)
                    sq_, sqk = sc32()
                    p.add("act", lambda e, sq_=sq_, bank=bank: e.activation(out=sq_[:, 0:512], in_=bank, func=AF.Silu), r=[bk], w=[sqk])
                    p.add("dve", lambda e, sq_=sq_, half=half: e.tensor_tensor(out=qt[:, hs(half)], in0=sq_[:, 0:512],
                                                                               in1=eb[:, hs(half)], op=ALU.mult),
                          r=[sqk, ("m", "eb", half)], w=[("A", "qt", db, half)])
                wv, wk = wtile128(4096 + h * 128) if not state else (None, None)
                for half in (range(2) if not state else ()):
                    bank, bk = proj_fm(wv, wk, half)
                    p.add("act", lambda e, bank=bank, half=half: e.activation(out=sgz[:, hs(half)], in_=bank, func=AF.Silu),
                          r=[bk], w=[("A", "sgz", db, half)])
                bi_, bank, bk = psb()
                bankb = bank.bitcast(BF16).rearrange("p (a b) -> p a b", b=128)
                for j in range(8):
                    vi_, vik = vis[j // 4]
                    p.add("pe", lambda e, j=j, vi_=vi_: e.transpose(out=bankb[:, j, :], in_=vi_[:, (j % 4) * 128:(j % 4 + 1) * 128],
                                                                    identity=ident_b[:]),
                          r=[vik, ("c", "ident_b")], w=[bk])
                eng = evac_eng()
                p.add(eng, copy_on(eng, vt[:, :, :], bankb), r=[bk], w=[("A", "vt", db, 0), ("A", "vt", db, 1)])

            sadd = {}

            def emit_s12(h):
                db = h % 2
                qt, vt = qt2[db], vt2[db]
                bi_, bank, bk = psb()
                bankb = bank.bitcast(BF16).rearrange("p (a b) -> p a b", b=128)
                for j in range(8):
                    p.add("pe", lambda e, j=j: e.transpose(out=bankb[:, j, :], in_=kh[:, j * 128:(j + 1) * 128], identity=ident_b[:]),
                          r=[("m", "kh", j // 4), ("c", "ident_b")], w=[bk])
                p.add("dve", lambda e: e.tensor_copy(out=kA[0:64, :, :], in_=bankb[0:64, :, :]), r=[bk], w=[("m", "kA")])
                p.add("dve", lambda e: e.tensor_copy(out=kB[64:128, :, :], in_=bankb[64:128, :, :]), r=[bk], w=[("m", "kB")])
                for jh in (range(2) if not state else ()):
                    bi_, bank, bk = psb()
                    for jj in range(4):
                        j = jh * 4 + jj
                        p.add("pe", lambda e, bank=bank, jj=jj, j=j: e.matmul(
                            bank[:, jj * 128:(jj + 1) * 128], lhsT=kt[:, j * 128:(j + 1) * 128], rhs=qt[:, j * 128:(j + 1) * 128],
                            start=True, stop=True), r=[("m", "kt", jh), ("A", "qt", db, jh)], w=[bk])
                    p.add("dve", lambda e, bank=bank, jh=jh: e.tensor_tensor(
                        out=Apair[:, jh * 4:jh * 4 + 4, :], in0=bank.rearrange("p (a b) -> p a b", b=128),
                        in1=cmask[:].rearrange("p (a b) -> p a b", a=1).to_broadcast([128, 4, 128]), op=ALU.mult),
                        r=[bk, ("c", "cmask")], w=[("m", "Ap", jh)])
                for q4 in range(4):
                    bi_, bank2, bk2 = psb()
                    for i in range(4):
                        n = q4 * 4 + i
                        j = n // 2
                        src = kA if n % 2 == 0 else kB
                        srk = ("m", "kA") if n % 2 == 0 else ("m", "kB")
                        p.add("pe", lambda e, bank2=bank2, i=i, j=j, src=src: e.matmul(
                            bank2[:, i * 128:(i + 1) * 128], lhsT=src[:, j, :], rhs=vt[:, j, :], start=True, stop=True),
                            r=[srk, ("A", "vt", db, j // 4)], w=[bk2])
                    sadd[q4] = (bank2, bk2)

            def emit_chain(h):
                prev = S_carry[:, h, :]
                prevk = ("c", "Sc", h)
                for n in range(16):
                    bank2, bk2 = sadd[n // 4]
                    if not state:
                        p.add("act", lambda e, prev=prev, n=n: e.copy(out=Sbf[:, n, :], in_=prev), r=[prevk], w=[("m", "Sbf", n)])
                    if n == 15:
                        nxt, nxk = S_carry[:, h, :], ("c", "Sc", h)
                    else:
                        nxt, nxk = Spp[:, n % 2, :], ("m", "Spp", n % 2)
                    col = n * 64 + 63
                    p.add("dve", lambda e, prev=prev, nxt=nxt, n=n, bank2=bank2, col=col: e.scalar_tensor_tensor(
                        out=nxt, in0=prev, scalar=eb[:, col:col + 1], in1=bank2[:, (n % 4) * 128:(n % 4 + 1) * 128],
                        op0=ALU.mult, op1=ALU.add), r=[prevk, ("m", "eb", n // 8), bk2], w=[nxk])
                    prev, prevk = nxt, nxk

            def emit_s45(h):
                db = h % 2
                qt, vt, sgz = qt2[db], vt2[db], sgz2[db]
                st1 = []
                for jh in range(2):
                    bi_, bank, bk = psb()
                    for jj in range(4):
                        j = jh * 4 + jj
                        o_ = bank[:, jj * 128:(jj + 1) * 128]
                        p.add("pe", lambda e, o_=o_, j=j: e.matmul(o_, lhsT=vt[:, j, :], rhs=Apair[:, j, :], start=True, stop=False),
                              r=[("A", "vt", db, jh), ("m", "Ap", jh)], w=[bk])
                        for i in range(2):
                            n = 2 * j + i
                            p.add("pe", lambda e, o_=o_, n=n, i=i: e.matmul(
                                o_[:, i * 64:(i + 1) * 64], lhsT=Sbf[:, n, :], rhs=qt[:, n * 64:(n + 1) * 64],
                                start=False, stop=(i == 1)), r=[("m", "Sbf", n), ("A", "qt", db, jh)], w=[bk])
                    o32, ok = sc32()
                    p.add("dve", lambda e, o32=o32, bank=bank: e.tensor_copy(out=o32[:, 0:512], in_=bank), r=[bk], w=[ok])
                    q_, qk = sc16()
                    p.add("act", lambda e, q_=q_, o32=o32: e.activation(out=q_[:], in_=o32[:, 0:512], func=AF.Square), r=[ok], w=[qk])
                    st1.append((o32, ok, q_, qk))
                for jh in range(2):
                    o32, ok, q_, qk = st1[jh]
                    bi_, b2, bk2 = psb()
                    p.add("pe", lambda e, b2=b2, q_=q_: e.matmul(b2, lhsT=ones_b[:], rhs=q_[:], start=True, stop=True),
                          r=[qk, ("c", "ones")], w=[bk2])
                    rs_, rk = rstd2[jh], ("c", "rstd", jh)
                    emit_rstd(b2, bk2, 1.0 / 128, rs_[:, 0:512], rk)
                    t_, tk = sc32()
                    p.add("dve", lambda e, t_=t_, o32=o32, rs_=rs_: e.scalar_tensor_tensor(
                        out=t_[:, 0:512], in0=o32[:, 0:512], scalar=gnw, in1=rs_[:, 0:512], op0=ALU.mult, op1=ALU.mult),
                        r=[ok, rk, ("c", "colsT")], w=[tk])
                    p.add("dve", lambda e, t_=t_, jh=jh: e.tensor_tensor(out=ymix[:, h, hs(jh)], in0=t_[:, 0:512],
                                                                         in1=sgz[:, hs(jh)], op=ALU.mult),
                          r=[tk, ("A", "sgz", db, jh)], w=[yk(h, jh)])

            if state:
                for h in range(8):
                    emit_proj(h)
                    emit_s12(h)
                    emit_chain(h)
                p.add("pool", lambda e: e.dma_start(out=cin_d[0:1024, :].rearrange("(h q) n -> q h n", q=128), in_=S_carry[:]),
                      r=[("c", "Sc", h) for h in range(8)], w=[("dram", "cin", 0)], dma=True)
                p.add("pool", lambda e: e.dma_start(out=cin_d[1024:1152, :], in_=halo_new.rearrange("p c t -> p (c t)")),
                      r=[("A", "halo_new", gi) for gi in range(4)], w=[("dram", "cin", 1)], dma=True)
                p.add("pool", lambda e: e.collective_compute("AllGather", ALU.bypass, replica_groups=[list(range(8))],
                                                             ins=[cin_d.opt()], outs=[cout_d.opt()]),
                      r=[("dram", "cin", 0), ("dram", "cin", 1)], w=[("dram", "cout")], dma="cc")
                for ri, r_ in enumerate((0, 2, 4, 6)):
                    base = r_ * 1152
                    p.add("pool", lambda e, base=base: e.dma_start(
                        out=gsc[:, 0:8, :], in_=cout_d[base:base + 1024, :].rearrange("(h q) n -> q h n", q=128)),
                        r=[("dram", "cout")], w=[("A", "gsc", 0)], dma=True)
                    p.add("pool", lambda e, base=base: e.dma_start(out=gsc[:, 8, :], in_=cout_d[base + 1024:base + 1152, :]),
                          r=[("dram", "cout")], w=[("A", "gsc", 1)], dma=True)
                    Sall = S_carry[:].rearrange("p h n -> p (h n)")
                    hall = halo[:].rearrange("p c t -> p (c t)")
                    gs_ = gsc[:, 0:8, :].rearrange("p h n -> p (h n)")
                    skeys = [("c", "Sc", h) for h in range(8)]
                    hkeys = [("c", "halo", c) for c in range(8)]
                    if ri == 0:
                        p.add("dve", lambda e, r_=r_: e.tensor_scalar(out=Sall, in0=gs_, scalar1=sel[:, r_:r_ + 1], scalar2=None, op0=ALU.mult),
                              r=[("A", "gsc", 0), ("c", "sel"), ("dram", "cin", 0)], w=skeys)
                        p.add("dve", lambda e, r_=r_: e.tensor_scalar(out=hall, in0=gsc[:, 8, :], scalar1=sel[:, r_:r_ + 1], scalar2=None, op0=ALU.mult),
                              r=[("A", "gsc", 1), ("c", "sel")], w=hkeys)
                    else:
                        p.add("dve", lambda e, r_=r_: e.scalar_tensor_tensor(out=Sall, in0=gs_, scalar=sel[:, r_:r_ + 1], in1=Sall,
                                                                             op0=ALU.mult, op1=ALU.add),
                              r=[("A", "gsc", 0), ("c", "sel")] + skeys, w=skeys)
                        p.add("dve", lambda e, r_=r_: e.scalar_tensor_tensor(out=hall, in0=gsc[:, 8, :], scalar=sel[:, r_:r_ + 1], in1=hall,
                                                                             op0=ALU.mult, op1=ALU.add),
                              r=[("A", "gsc", 1), ("c", "sel")] + hkeys, w=hkeys)
                return
            emit_proj(0)
            if stop == 32:
                return
            for h in range(8):
                emit_s12(h)
                if stop == 33:
                    return
                emit_chain(h)
                if stop == 34:
                    return
                if h + 1 < 8:
                    emit_proj(h + 1)
                emit_s45(h)
                if stop == 35 or stop == 40 + h:
                    return
            emit_wout(ymix, 1, yk)
            emit_pool_branch()
            emit_wout(ymix, 0, yk)

        def dump_x():
            for kc in range(KC):
                p.add("sp", lambda e, kc=kc: e.dma_start(out=dbg_d[:, kc, :], in_=xT[:, kc, :]),
                      r=[xk(kc, 0), xk(kc, 1)], w=[("o", "dbg", kc)], dma=True)

        def run_all():
            for blk in range(nblk):
                p.switch("A")
                emit_load_x(blk)
                if exch:
                    emit_norm_mod(0, part=1)
                    emit_mod_all()
                if stop == 1:
                    return
                if blk == 0 and not exch:
                    for seg in (0, 1, 2):
                        emit_mod2(seg)
                if stop == 11:
                    p.add("dve", lambda e: e.tensor_copy(out=xT[:, 0, 0:144], in_=vecs[:].rearrange("p a b -> p (a b)")),
                          r=[("c", "vec", s_) for s_ in range(3)], w=[xk(0, 0)])
                    p.add("dve", lambda e: e.tensor_copy(out=xT[:, 1, 0:144], in_=modT[:]),
                          r=[("c", "mod", s_) for s_ in range(3)], w=[xk(1, 0)])
                    p.add("dve", lambda e: e.tensor_copy(out=xT[:, 2, 0:128], in_=colsT[:]), r=[("c", "colsT")], w=[xk(2, 0)])
                    p.add("dve", lambda e: e.tensor_copy(out=xT[:, 3, 0:144], in_=badaT[:]), r=[("c", "badaT", 0), ("c", "badaT", 128)], w=[xk(3, 0)])
                    p.add("dve", lambda e: e.tensor_copy(out=xT[:, 4, 0:16], in_=cT[:]), r=[("c", "cT")], w=[xk(4, 0)])
                    return
                emit_norm_mod(0, part=(2 if exch else 3))
                if stop == 12:
                    for kc_ in range(KC):
                        for hf_ in range(2):
                            p.add("act", lambda e: e.copy(out=xT[:, kc_, hs(hf_)], in_=hT[:, kc_, hs(hf_)]),
                                  r=[hk(kc_, hf_)], w=[xk(kc_, hf_)])
                    return
                p.switch("A")
                emit_ffn(0, 0)
                if stop == 2:
                    return
                if blk == 0 and not exch:
                    for seg in (3, 4, 5):
                        emit_mod2(seg)
                emit_norm_mod(1)
                if exch:
                    p.switch("A")
                    emit_mixer(blk, state=True)
                p.switch("A")
                emit_mixer(blk)
                if stop == 3 or 30 < stop < 50:
                    return
                if blk == 0 and not exch:
                    for seg in (6, 7, 8):
                        emit_mod2(seg)
                emit_norm_mod(2)
                p.switch("A")
                emit_ffn(1, 2)
                if stop == 4:
                    return
                p.switch("A")
                emit_final(blk)

        run_all()
        if dbg:
            dump_x()
        okeys = [k for k in p.last_w if k[0] == "o"]
        p.add("sp", None, r=okeys)
        p.emit(nc, st)
    return nc, p


def _consts():
    ident = np.eye(128, dtype=np.float32)
    s = np.arange(128)[:, None]
    t = np.arange(128)[None, :]
    cmask = ((s // 64 == t // 64) & (s <= t)).astype(np.float32)
    smask = np.ones((128, 512), np.float32)
    smask[:, ::64] = 0.0
    invcnt = np.zeros((128, 64), np.float32)
    for gi, w in enumerate(WIN):
        invcnt[:, gi * 16:(gi + 1) * 16] = 1.0 / np.minimum(np.arange(16) + 1, w).astype(np.float32)
    return ident, cmask, smask, invcnt


_CACHE = {}


def make_in_maps8(inp):
    maps4 = make_in_maps(inp, 2, 4)
    maps = []
    for i in range(8):
        b, hh = i // 2, i % 2
        m = dict(maps4[b])
        m["x"] = np.ascontiguousarray(maps4[b]["x"][hh * T:(hh + 1) * T])
        sel = np.zeros((128, 8), np.float32)
        if hh == 1:
            sel[:, i - 1] = 1.0
            ic = np.zeros((128, 64), np.float32)
            for gi, w in enumerate(WIN):
                ic[:, gi * 16:(gi + 1) * 16] = 1.0 / w
            m["invcnt"] = ic
        m["sel"] = sel
        del m["w_ada"]
        m["w_ada_sl"] = np.ascontiguousarray(maps4[0]["w_ada"][:, i * 2304:(i + 1) * 2304])
        m["c_all"] = np.ascontiguousarray(np.asarray(inp["c"], dtype=np.float32).reshape(64, 128))
        sb_ = np.zeros((128, 4), np.float32)
        sb_[:, b] = 1.0
        m["selb"] = sb_
        maps.append(m)
    return maps


def make_in_maps(inp, nblk=2, ncores=4):
    ident, cmask, smask, invcnt = _consts()
    f = lambda a: np.ascontiguousarray(np.asarray(a, dtype=np.float32))
    shared = {
        "w_ada": f(inp["w_ada"][0]),
        "b_ada": f(inp["b_ada"][0]).reshape(144, 128),
        "nrm": f(np.stack([inp["norm1_w"][0], inp["norm2_w"][0], inp["norm3_w"][0], inp["final_norm_w"]])).reshape(4, 16, 128),
        "ffn1_gate": f(inp["ffn1_gate"][0]), "ffn1_up": f(inp["ffn1_up"][0]), "ffn1_down": f(inp["ffn1_down"][0]),
        "ffn2_gate": f(inp["ffn2_gate"][0]), "ffn2_up": f(inp["ffn2_up"][0]), "ffn2_down": f(inp["ffn2_down"][0]),
        "w_in": f(inp["w_in"][0]),
        "pool_w": f(inp["pool_w"][0]),
        "pool_scale": f(inp["pool_scale"][0]).reshape(8, 128),
        "lb_logits": f(inp["lb_logits"]).reshape(16, 128),
        "gnorm_w": f(inp["gnorm_w"][0]).reshape(1, 128),
        "w_out": f(inp["w_out"][0]),
        "ident": ident, "cmask": cmask, "smask": smask, "invcnt": invcnt,
    }
    maps = []
    for i in range(ncores):
        m = dict(shared)
        m["x"] = f(inp["x"][i])[: nblk * T]
        m["sel"] = np.zeros((128, 8), np.float32)
        m["c"] = f(inp["c"][i]).reshape(16, 128)
        maps.append(m)
    return maps


def kernel(**inputs):
    if "nc" not in _CACHE:
        _CACHE["nc"] = build_nc(1, exch=True)[0]
    nc = _CACHE["nc"]
    maps = make_in_maps8(inputs)
    res = run_bass_kernel_spmd(nc, maps, core_ids=list(range(8)))
    out = np.stack([np.asarray(r["out"], dtype=np.float32) for r in res.results], axis=0)
    return np.ascontiguousarray(out.reshape(4, S, D))
```
